# Optimizing a Trainium2 kernel written in Bass

```python
import math
import jax, jax.numpy as jnp
from jax import lax
import numpy as np

D_MODEL = 1024
BATCH = 32
SEQ = 256
DEPTH = 4
DEC_BATCH = 8
DEC_SEQ = 4096
PAST_LEN = 256

GRID_W = 64
MIX_WIDTH = D_MODEL
POOL_WIDTH = MIX_WIDTH // 4
POOL_WINDOWS = (2, 4, 8, 16)
POOL_GROUPS = len(POOL_WINDOWS)
POOL_GROUP_DIM = POOL_WIDTH // POOL_GROUPS
CONV_WIDTH = MIX_WIDTH // 4
GQA_WIDTH = MIX_WIDTH // 4
GQA_HEAD_DIM = 64
GQA_HEADS = GQA_WIDTH // GQA_HEAD_DIM
GQA_KV_HEADS = 2
GQA_GROUP = GQA_HEADS // GQA_KV_HEADS
DIFF_WIDTH = MIX_WIDTH - POOL_WIDTH - CONV_WIDTH - GQA_WIDTH
DIFF_HEADS = 4
DIFF_HEAD_DIM = DIFF_WIDTH // DIFF_HEADS // 2
FFN_DIM = 2816
N_MOD = 9
ROPE_THETA = 10000.0
Q_BLOCK = 128
NORM_EPS = 1e-6

IN_SIZES = (POOL_WIDTH,
            CONV_WIDTH, CONV_WIDTH, CONV_WIDTH,
            GQA_HEADS * GQA_HEAD_DIM,
            GQA_KV_HEADS * GQA_HEAD_DIM,
            GQA_KV_HEADS * GQA_HEAD_DIM,
            DIFF_HEADS * 2 * DIFF_HEAD_DIM,
            DIFF_HEADS * 2 * DIFF_HEAD_DIM,
            DIFF_HEADS * 2 * DIFF_HEAD_DIM)
MIX_IN = int(sum(IN_SIZES))
IN_SPLITS = [int(v) for v in np.cumsum(IN_SIZES)[:-1]]

kernel_name = "hybrid_pool_conv_gqa_diffattn_dit_step"


def rmsnorm(x, g):
    xf = x.astype(jnp.float32)
    y = xf * lax.rsqrt(jnp.mean(xf * xf, axis=-1, keepdims=True) + NORM_EPS)
    return (y * g.astype(jnp.float32)).astype(x.dtype)


def axial_rope(seq, dim):
    rows = seq // GRID_W
    row = jnp.broadcast_to(jnp.arange(rows)[:, None], (rows, GRID_W)).reshape(seq).astype(jnp.float32)
    col = jnp.broadcast_to(jnp.arange(GRID_W)[None, :], (rows, GRID_W)).reshape(seq).astype(jnp.float32)
    quarter = dim // 4
    inv = 1.0 / (ROPE_THETA ** (jnp.arange(quarter, dtype=jnp.float32) / quarter))
    ar = row[:, None] * inv
    ac = col[:, None] * inv
    ang = jnp.concatenate([ar, ar, ac, ac], axis=-1)
    return jnp.cos(ang), jnp.sin(ang)


def apply_rope(x, cs):
    cos, sin = cs
    x1, x2, x3, x4 = jnp.split(x, 4, axis=-1)
    rot = jnp.concatenate([-x2, x1, -x4, x3], axis=-1)
    return (x.astype(jnp.float32) * cos + rot.astype(jnp.float32) * sin).astype(x.dtype)


def modulate(h, shift, scale):
    return h * (1.0 + scale) + shift


def swiglu(h, w_in, w_out):
    g, u = jnp.split(h @ w_in, 2, axis=-1)
    return (jax.nn.silu(g) * u) @ w_out


def pool_mix(u, w_grp, scale):
    b, s, _ = u.shape
    uf = u.astype(jnp.float32)
    cs = jnp.concatenate([jnp.zeros_like(uf[:, :1]), jnp.cumsum(uf, axis=1)], axis=1)
    t = jnp.arange(s)
    u_groups = jnp.split(uf, POOL_GROUPS, axis=-1)
    cs_groups = jnp.split(cs, POOL_GROUPS, axis=-1)
    outs = []
    for w, ug, cg in zip(POOL_WINDOWS, u_groups, cs_groups):
        lo = jnp.clip(t - w // 2, 0, s)
        hi = jnp.clip(t + w // 2, 0, s)
        total = jnp.take(cg, hi, axis=1) - jnp.take(cg, lo, axis=1)
        cnt = (hi - lo).astype(jnp.float32)[None, :, None]
        outs.append(total / cnt - ug)
    p = jnp.stack(outs, axis=2).astype(u.dtype)
    y = jnp.einsum('bsgc,gcd->bsgd', p, w_grp).reshape(b, s, POOL_WIDTH)
    return y * scale


def conv_mix(h, bg, cg, w):
    u = cg * h
    up = jnp.pad(u, ((0, 0), (1, 1), (0, 0)))
    conv = up[:, :-2] * w[0] + up[:, 1:-1] * w[1] + up[:, 2:] * w[2]
    return bg * conv


def gqa_attend(q, k, v):
    b, hk, g, s, d = q.shape
    nb = s // Q_BLOCK
    qb = q.reshape(b, hk, g, nb, Q_BLOCK, d).transpose(3, 0, 1, 2, 4, 5)
    sc_mult = d ** -0.5

    def one(qi):
        sc = jnp.einsum('bhgqd,bhkd->bhgqk', qi, k).astype(jnp.float32) * sc_mult
        p = jax.nn.softmax(sc, axis=-1).astype(v.dtype)
        return jnp.einsum('bhgqk,bhkd->bhgqd', p, v)

    o = lax.map(one, qb)
    return o.transpose(1, 2, 3, 0, 4, 5).reshape(b, hk, g, s, d)


def diff_attend(q, k, v, lam):
    b, h, _, s, d = q.shape
    nb = s // Q_BLOCK
    qb = q.reshape(b, h, 2, nb, Q_BLOCK, d).transpose(3, 0, 1, 2, 4, 5)
    sc_mult = d ** -0.5

    def one(qi):
        sc = jnp.einsum('bhcqd,bhckd->bhcqk', qi, k).astype(jnp.float32) * sc_mult
        p = jax.nn.softmax(sc, axis=-1)
        a = (p[:, :, 0] - lam * p[:, :, 1]).astype(v.dtype)
        return jnp.einsum('bhqk,bhkd->bhqd', a, v)

    o = lax.map(one, qb)
    return o.transpose(1, 2, 0, 3, 4).reshape(b, h, s, v.shape[-1])


def token_mix(h, p, lam_init, ropes, ctx):
    b, s, _ = h.shape
    proj = h @ p['w_mix_in']
    u_pool, h_conv, b_conv, c_conv, q_g, k_g, v_g, q_d, k_d, v_d = jnp.split(proj, IN_SPLITS, axis=-1)

    y_pool = pool_mix(u_pool, p['pool_w'], p['pool_scale'])
    y_conv = conv_mix(h_conv, b_conv, c_conv, p['conv_w'])

    q_g = rmsnorm(q_g.reshape(b, s, GQA_HEADS, GQA_HEAD_DIM), p['gqa_q_norm']).transpose(0, 2, 1, 3)
    k_g = rmsnorm(k_g.reshape(b, s, GQA_KV_HEADS, GQA_HEAD_DIM), p['gqa_k_norm']).transpose(0, 2, 1, 3)
    v_g = v_g.reshape(b, s, GQA_KV_HEADS, GQA_HEAD_DIM).transpose(0, 2, 1, 3)
    q_d = q_d.reshape(b, s, DIFF_HEADS, 2, DIFF_HEAD_DIM).transpose(0, 2, 3, 1, 4)
    k_d = k_d.reshape(b, s, DIFF_HEADS, 2, DIFF_HEAD_DIM).transpose(0, 2, 3, 1, 4)
    v_d = v_d.reshape(b, s, DIFF_HEADS, 2 * DIFF_HEAD_DIM).transpose(0, 2, 1, 3)

    kv_out = (k_g, v_g,
              k_d.transpose(0, 1, 3, 2, 4).reshape(b, DIFF_HEADS, s, 2 * DIFF_HEAD_DIM), v_d)

    if ctx is None:
        keys_g, vals_g, keys_d, vals_d = k_g, v_g, k_d, v_d
    else:
        rope_g, rope_d = ropes
        ck_g, cv_g, ck_d, cv_d = ctx
        n_ctx = ck_d.shape[2]
        ck_d = ck_d.reshape(b, DIFF_HEADS, n_ctx, 2, DIFF_HEAD_DIM).transpose(0, 1, 3, 2, 4)
        q_g = apply_rope(q_g, rope_g)
        q_d = apply_rope(q_d, rope_d)
        keys_g = jnp.concatenate([ck_g, apply_rope(k_g, rope_g)], axis=2)
        vals_g = jnp.concatenate([cv_g, v_g], axis=2)
        keys_d = jnp.concatenate([ck_d, apply_rope(k_d, rope_d)], axis=3)
        vals_d = jnp.concatenate([cv_d, v_d], axis=2)

    o_g = gqa_attend(q_g.reshape(b, GQA_KV_HEADS, GQA_GROUP, s, GQA_HEAD_DIM), keys_g, vals_g)
    y_gqa = o_g.reshape(b, GQA_HEADS, s, GQA_HEAD_DIM).transpose(0, 2, 1, 3).reshape(b, s, GQA_WIDTH)

    lv = p['diff_lambda'].astype(jnp.float32)
    lam = jnp.exp(jnp.sum(lv[0] * lv[1])) - jnp.exp(jnp.sum(lv[2] * lv[3])) + lam_init
    o_d = diff_attend(q_d, keys_d, vals_d, lam)
    o_d = rmsnorm(o_d, p['diff_subln']) * (1.0 - lam_init)
    y_diff = o_d.transpose(0, 2, 1, 3).reshape(b, s, DIFF_WIDTH)

    y = jnp.concatenate([y_pool, y_conv, y_gqa, y_diff], axis=-1) @ p['w_mix_out']
    return y, kv_out


def layer(x, cond, p, lam_init, ropes, ctx):
    mods = (jax.nn.silu(cond) @ p['w_ada'] + p['b_ada']).reshape(cond.shape[0], N_MOD, 1, D_MODEL)
    sh1, sc1, g1, sh2, sc2, g2, sh3, sc3, g3 = [mods[:, i] for i in range(N_MOD)]
    h = modulate(rmsnorm(x, p['norm_ffn1']), sh1, sc1)
    x = x + 0.5 * g1 * swiglu(h, p['w_ffn1_in'], p['w_ffn1_out'])
    h = modulate(rmsnorm(x, p['norm_mix']), sh2, sc2)
    y, kv = token_mix(h, p, lam_init, ropes, ctx)
    x = x + g2 * y
    h = modulate(rmsnorm(x, p['norm_ffn2']), sh3, sc3)
    x = x + 0.5 * g3 * swiglu(h, p['w_ffn2_in'], p['w_ffn2_out'])
    return x, kv


def setup_inputs(seed: int = 0) -> dict:
    key = jax.random.key(seed)
    ks = jax.random.split(key, 32)
    f32 = jnp.float32
    nrm = lambda k, shape, s: (jax.random.normal(k, shape, f32) * s)
    D, L = D_MODEL, DEPTH
    return {
        "x_prompt": nrm(ks[0], (BATCH, SEQ, D), 1.0),
        "x_sample": nrm(ks[1], (DEC_BATCH, DEC_SEQ, D), 1.0),
        "cache_gqa_k": nrm(ks[2], (DEC_BATCH, L, GQA_KV_HEADS, PAST_LEN, GQA_HEAD_DIM), 1.0),
        "cache_gqa_v": nrm(ks[3], (DEC_BATCH, L, GQA_KV_HEADS, PAST_LEN, GQA_HEAD_DIM), 1.0),
        "cache_diff_k": nrm(ks[4], (DEC_BATCH, L, DIFF_HEADS, PAST_LEN, 2 * DIFF_HEAD_DIM), 1.0),
        "cache_diff_v": nrm(ks[5], (DEC_BATCH, L, DIFF_HEADS, PAST_LEN, 2 * DIFF_HEAD_DIM), 1.0),
        "c": nrm(ks[6], (DEC_BATCH, D), 1.0),
        "c_ctx": nrm(ks[7], (D,), 1.0),
        "w_ada": nrm(ks[8], (L, D, N_MOD * D), 0.5 * D ** -0.5),
        "b_ada": nrm(ks[9], (L, N_MOD * D), 0.02),
        "norm_ffn1": 1.0 + nrm(ks[10], (L, D), 0.05),
        "norm_mix": 1.0 + nrm(ks[11], (L, D), 0.05),
        "norm_ffn2": 1.0 + nrm(ks[12], (L, D), 0.05),
        "w_ffn1_in": nrm(ks[13], (L, D, 2 * FFN_DIM), D ** -0.5),
        "w_ffn1_out": nrm(ks[14], (L, FFN_DIM, D), FFN_DIM ** -0.5),
        "w_ffn2_in": nrm(ks[15], (L, D, 2 * FFN_DIM), D ** -0.5),
        "w_ffn2_out": nrm(ks[16], (L, FFN_DIM, D), FFN_DIM ** -0.5),
        "w_mix_in": nrm(ks[17], (L, D, MIX_IN), D ** -0.5),
        "pool_w": nrm(ks[18], (L, POOL_GROUPS, POOL_GROUP_DIM, POOL_GROUP_DIM), POOL_GROUP_DIM ** -0.5),
        "pool_scale": 1.0 + nrm(ks[19], (L, POOL_WIDTH), 0.1),
        "conv_w": nrm(ks[20], (L, 3, CONV_WIDTH), 3 ** -0.5),
        "gqa_q_norm": 1.0 + nrm(ks[21], (L, GQA_HEAD_DIM), 0.05),
        "gqa_k_norm": 1.0 + nrm(ks[22], (L, GQA_HEAD_DIM), 0.05),
        "diff_lambda": nrm(ks[23], (L, 4, DIFF_HEAD_DIM), 0.1),
        "diff_subln": 1.0 + nrm(ks[24], (L, 2 * DIFF_HEAD_DIM), 0.05),
        "w_mix_out": nrm(ks[25], (L, MIX_WIDTH, D), MIX_WIDTH ** -0.5),
        "final_norm": 1.0 + nrm(ks[26], (D,), 0.05),
    }


def reference(x_prompt, x_sample, cache_gqa_k, cache_gqa_v, cache_diff_k, cache_diff_v, c, c_ctx,
              w_ada, b_ada, norm_ffn1, norm_mix, norm_ffn2, w_ffn1_in, w_ffn1_out, w_ffn2_in, w_ffn2_out,
              w_mix_in, pool_w, pool_scale, conv_w, gqa_q_norm, gqa_k_norm, diff_lambda, diff_subln,
              w_mix_out, final_norm):
    def params(l):
        return dict(w_ada=w_ada[l], b_ada=b_ada[l], norm_ffn1=norm_ffn1[l], norm_mix=norm_mix[l],
                    norm_ffn2=norm_ffn2[l], w_ffn1_in=w_ffn1_in[l], w_ffn1_out=w_ffn1_out[l],
                    w_ffn2_in=w_ffn2_in[l], w_ffn2_out=w_ffn2_out[l], w_mix_in=w_mix_in[l],
                    pool_w=pool_w[l], pool_scale=pool_scale[l], conv_w=conv_w[l],
                    gqa_q_norm=gqa_q_norm[l], gqa_k_norm=gqa_k_norm[l], diff_lambda=diff_lambda[l],
                    diff_subln=diff_subln[l], w_mix_out=w_mix_out[l])

    lam_inits = [0.8 - 0.6 * math.exp(-0.3 * l) for l in range(DEPTH)]

    h = x_prompt
    cond_ctx = c_ctx[None, :]
    ks_g, vs_g, ks_d, vs_d = [], [], [], []
    for l in range(DEPTH):
        h, kv = layer(h, cond_ctx, params(l), lam_inits[l], None, None)
        ks_g.append(kv[0]); vs_g.append(kv[1]); ks_d.append(kv[2]); vs_d.append(kv[3])
    y_prompt = rmsnorm(h, final_norm)
    new_gqa_k = jnp.stack(ks_g, axis=1)
    new_gqa_v = jnp.stack(vs_g, axis=1)
    new_diff_k = jnp.stack(ks_d, axis=1)
    new_diff_v = jnp.stack(vs_d, axis=1)

    s_lat = x_sample.shape[1]
    ropes = (axial_rope(s_lat, GQA_HEAD_DIM), axial_rope(s_lat, DIFF_HEAD_DIM))
    z = x_sample
    for l in range(DEPTH):
        ctx = (cache_gqa_k[:, l], cache_gqa_v[:, l], cache_diff_k[:, l], cache_diff_v[:, l])
        z, _ = layer(z, c, params(l), lam_inits[l], ropes, ctx)
    y_sample = rmsnorm(z, final_norm)

    return (y_prompt, y_sample, new_gqa_k, new_gqa_v, new_diff_k, new_diff_v)
```

```python
import math
from contextlib import ExitStack

import numpy as np
import concourse.bass as bass
import concourse.mybir as mybir
from concourse.bass_utils import run_bass_kernel_spmd

F32 = mybir.dt.float32
BF16 = mybir.dt.bfloat16
AF = mybir.ActivationFunctionType
ALU = mybir.AluOpType
AX = mybir.AxisListType

ENGS = ("pe", "act", "dve", "pool", "sp")
DEBUG = False
KEEP_WARM = 1
P = 128
D = 1024
DC = 8
T = 512
FF = 2816
FC = 22
EPS = 1e-6
N_MOD = 9


class Buf:
    __slots__ = ("name", "w", "r")

    def __init__(self, name=""):
        self.name = name
        self.w = None
        self.r = {}


class Chan:
    __slots__ = ("key", "count")

    def __init__(self, key):
        self.key = key
        self.count = 0


class Sched:
    def __init__(self):
        self.streams = {e: [] for e in ENGS}
        self.cnt = {e: 0 for e in ENGS}
        self.seen = {e: {} for e in ENGS}
        self.chans = []
        self.chan_by_key = {}
        self.inter_waits = {}
        self.n_instr = 0

    def chan(self, name):
        c = Chan("ch%d_%s" % (len(self.chans), name))
        self.chans.append(c)
        self.chan_by_key[c.key] = c
        return c

    def _need(self, eng, tok, waits):
        if tok is None:
            return
        k, v = tok
        if eng == "pe" and k == "e_pe":
            return
        if self.seen[eng].get(k, 0) >= v:
            return
        if waits.get(k, 0) < v:
            waits[k] = v

    def _deps(self, eng, reads, writes):
        waits = {}
        for b in reads:
            self._need(eng, b.w, waits)
        for b in writes:
            self._need(eng, b.w, waits)
            for k, v in b.r.items():
                self._need(eng, (k, v), waits)
        for k, v in waits.items():
            self.seen[eng][k] = v
            self.streams[eng].append(("wait", k, v))
            c = self.chan_by_key.get(k)
            if c is not None and c.count > v:
                self.inter_waits[k] = self.inter_waits.get(k, 0) + 1

    @staticmethod
    def _commit(tok, reads, writes):
        k, v = tok
        for b in reads:
            if b.r.get(k, 0) < v:
                b.r[k] = v
        for b in writes:
            b.w = tok
            b.r = {}

    def op(self, eng, fn, reads=(), writes=(), inc=True):
        self.n_instr += 1
        self._deps(eng, reads, writes)
        if inc:
            self.cnt[eng] += 1
            tok = ("e_" + eng, self.cnt[eng])
            self.streams[eng].append(("op", fn, tok))
        else:
            self.streams[eng].append(("op", fn, None))
            tok = ("e_" + eng, self.cnt[eng] + 1)
        self._commit(tok, reads, writes)

    def dma(self, eng, chan, fn, reads=(), writes=()):
        self.n_instr += 1
        self._deps(eng, reads, writes)
        chan.count += 16
        tok = (chan.key, chan.count)
        self.streams[eng].append(("dma", fn, tok))
        self._commit(tok, reads, writes)

    def wait_all(self, eng, bufs):
        self._deps(eng, (), bufs)

    def barrier(self):
        toks = [("e_" + e, self.cnt[e]) for e in ENGS if self.cnt[e] > 0]
        toks += [(c.key, c.count) for c in self.chans if c.count > 0]
        for e in ENGS:
            for k, v in toks:
                if e == "pe" and k == "e_pe":
                    continue
                if self.seen[e].get(k, 0) < v:
                    self.seen[e][k] = v
                    self.streams[e].append(("wait", k, v))

    def sem_keys(self):
        keys = ["e_" + e for e in ENGS if self.cnt[e] > 0]
        keys += [c.key for c in self.chans if c.count > 0]
        return keys

    def emit(self, block, sems):
        names = {"pe": "tensor", "act": "scalar", "dve": "vector", "pool": "gpsimd", "sp": "sync"}
        for ename in ENGS:
            stream = self.streams[ename]
            if not stream:
                continue

            def _f(eobj, stream=stream):
                for item in stream:
                    if item[0] == "wait":
                        eobj.wait_ge(sems[item[1]], item[2])
                    else:
                        ins = item[1](eobj)
                        if item[2] is not None:
                            ins.then_inc(sems[item[2][0]], 16 if item[0] == "dma" else 1)
            getattr(block, names[ename])(_f)


class Ring:
    def __init__(self, slots):
        self.slots = slots
        self.i = 0

    def next(self):
        s = self.slots[self.i % len(self.slots)]
        self.i += 1
        return s


def _rope_tables(seq, dim, reps):
    grid_w = 64
    rows = seq // grid_w
    row = np.broadcast_to(np.arange(rows, dtype=np.float32)[:, None], (rows, grid_w)).reshape(seq)
    col = np.broadcast_to(np.arange(grid_w, dtype=np.float32)[None, :], (rows, grid_w)).reshape(seq)
    quarter = dim // 4
    inv = (1.0 / (np.float32(10000.0) ** (np.arange(quarter, dtype=np.float32) / np.float32(quarter)))).astype(np.float32)
    ar = row[:, None] * inv
    ac = col[:, None] * inv
    ang = np.concatenate([ar, ar, ac, ac], axis=-1).astype(np.float32)
    cos = np.cos(ang).astype(np.float32).T
    sin = np.sin(ang).astype(np.float32).T
    return np.ascontiguousarray(np.tile(cos, (reps, 1))), np.ascontiguousarray(np.tile(sin, (reps, 1)))


def _rot_lhsT(dim):
    q = dim // 4
    R = np.zeros((dim, dim), np.float32)
    for i in range(q):
        R[i, i + q] = -1.0
        R[i + q, i] = 1.0
        R[i + 2 * q, i + 3 * q] = -1.0
        R[i + 3 * q, i + 2 * q] = 1.0
    full = np.zeros((128, 128), np.float32)
    for b in range(128 // dim):
        full[b * dim:(b + 1) * dim, b * dim:(b + 1) * dim] = R
    return np.ascontiguousarray(full.T)


def _const_pack(seq_len_for_pool):
    ident = np.eye(128, dtype=np.float32)
    ones = np.ones((128, 128), np.float32)
    bd64 = np.zeros((128, 128), np.float32)
    bd64[:64, :64] = 1.0
    bd64[64:, 64:] = 1.0
    rotg = _rot_lhsT(64)
    rotd = _rot_lhsT(32)
    wins = (2, 4, 8, 16)
    invw = np.zeros((128, 2), np.float32)
    icl = np.zeros((128, 2, 8), np.float32)
    icr = np.zeros((128, 2, 8), np.float32)
    s = seq_len_for_pool
    for c in range(2):
        for hf in range(2):
            w = wins[2 * c + hf]
            sl = slice(hf * 64, (hf + 1) * 64)
            invw[sl, c] = 1.0 / w
            for i in range(8):
                t = i
                lo = max(t - w // 2, 0); hi = min(t + w // 2, s)
                icl[sl, c, i] = 1.0 / (hi - lo)
                t = s - 8 + i
                lo = max(t - w // 2, 0); hi = min(t + w // 2, s)
                icr[sl, c, i] = 1.0 / (hi - lo)
    return np.ascontiguousarray(np.concatenate(
        [ident, ones, bd64, rotg, rotd, invw, icl.reshape(128, 16), icr.reshape(128, 16)], axis=1))


C_IDENT, C_ONES, C_BD64, C_ROTG, C_ROTD = 0, 128, 256, 384, 512
C_INVW, C_ICL, C_ICR, NCF = 640, 642, 658, 674


def build_program(L, S_S, NPS):
    assert S_S % T == 0 and NPS % 2 == 0
    NTOK_P = NPS * 256
    NTOK = S_S + NTOK_P
    NK_S = 256 + S_S
    NKMAX = max(NK_S, NTOK_P)
    NKC = NKMAX // 128
    VW = 66
    lam_inits = [0.8 - 0.6 * math.exp(-0.3 * l) for l in range(L)]

    nc = bass.Bass("TRN2", target_bir_lowering=False)
    S = Sched()

    def din(name, shape, dt=F32):
        return nc.dram_tensor(name, list(shape), dt, kind="ExternalInput").ap()

    def dout(name, shape):
        return nc.dram_tensor(name, list(shape), F32, kind="ExternalOutput").ap()

    def dscr(name, shape, dt):
        return nc.dram_tensor(name, list(shape), dt, kind="Internal").ap()

    xs_in = din("xs", [S_S, D]); xp_in = din("xp", [NTOK_P, D])
    ckg = din("ckg", [L, 2, 256, 64]); cvg = din("cvg", [L, 2, 256, 64])
    ckd = din("ckd", [L, 4, 256, 64]); cvd = din("cvd", [L, 4, 256, 64])
    cond = din("cond", [2, D])
    w_ada = din("w_ada", [L, D, N_MOD * D]); b_ada = din("b_ada", [L, N_MOD * D])
    norm_ffn1 = din("norm_ffn1", [L, D]); norm_mix = din("norm_mix", [L, D]); norm_ffn2 = din("norm_ffn2", [L, D])
    w_ffn_in = [din("w_ffn1_in", [L, D, 2 * FF]), din("w_ffn2_in", [L, D, 2 * FF])]
    w_ffn_out = [din("w_ffn1_out", [L, FF, D]), din("w_ffn2_out", [L, FF, D])]
    w_mix_in = din("w_mix_in", [L, D, 2304])
    pool_w = din("pool_w", [L, 4, 64, 64]); pool_scale = din("pool_scale", [L, 256]); conv_w = din("conv_w", [L, 3, 256])
    gqa_q_norm = din("gqa_q_norm", [L, 64]); gqa_k_norm = din("gqa_k_norm", [L, 64])
    diff_lambda = din("diff_lambda", [L, 4, 32]); diff_subln = din("diff_subln", [L, 64])
    w_mix_out = din("w_mix_out", [L, D, D]); final_norm = din("final_norm", [D])
    consts_in = din("consts", [P, NCF])
    ropes_in = {k: din("rope_" + k, [P, S_S]) for k in ("cg", "sg", "cd", "sd")}

    ys_out = dout("ys", [S_S, D]); yp_out = dout("yp", [NTOK_P, D])
    nkg = dout("nkg", [NPS, L, 2, 256, 64]); nvg = dout("nvg", [NPS, L, 2, 256, 64])
    nkd = dout("nkd", [NPS, L, 4, 256, 64]); nvd = dout("nvd", [NPS, L, 4, 256, 64])

    XS = dscr("XS", [D, NTOK], F32)
    QG = dscr("QG", [256, NTOK], BF16); QD = dscr("QD", [256, NTOK], BF16)
    UPs = dscr("UPs", [256, NTOK], F32); CHs = dscr("CHs", [256, NTOK], F32); Bs = dscr("Bs", [256, NTOK], F32)
    n_tiles = NTOK // T
    XS_b = [Buf("XS%d" % i) for i in range(n_tiles)]
    SCR = {k: [Buf("SCR%s%d" % (k, i)) for i in range(n_tiles)] for k in ("up", "ch", "b", "qg", "qd")}

    with ExitStack() as es:
        def sb(name, shape, dt):
            return es.enter_context(nc.sbuf_tensor(name, list(shape), dt))

        cf = sb("cf", [P, NCF], F32); cf_b = Buf("cf")
        cb = sb("cb", [P, 384], BF16); cb_b = Buf("cb")
        ident = cf[:, C_IDENT:C_IDENT + 128]
        ones_b = cb[:, 0:128]; bd64_b = cb[:, 128:256]
        rotg = cf[:, C_ROTG:C_ROTG + 128]; rotd = cf[:, C_ROTD:C_ROTD + 128]
        onesf = cf[:, C_ONES:C_ONES + 128]
        mods = sb("mods", [P, L, 72, 2], F32); mods_b = Buf("mods")
        bada = sb("bada", [P, L, 72], F32)
        ngain = sb("ngain", [P, L, 3, DC], F32)
        fgain = sb("fgain", [P, DC], F32)
        modA = sb("modA", [P, L, 3, DC, 2], F32)
        modG = sb("modG", [P, L, 3, DC, 2], F32)
        qkn = sb("qkn", [P, 2, L], F32)
        subln = sb("subln", [64, L], F32)
        pscale = sb("pscale", [P, L, 2], F32)
        convw = sb("convw", [P, L, 3, 2], F32)
        bdw = sb("bdw", [P, L, 2, 128], BF16)
        nlam = sb("nlam", [64, L], F32)
        dl = sb("dl", [1, L, 4, 32], F32)
        dl2 = sb("dl2", [1, L, 2, 32], F32)
        dl3 = sb("dl3", [1, L * 2], F32)
        condT = sb("condT", [P, DC, 2], F32)
        condS = sb("condS", [P, DC, 2], BF16)
        par_b = Buf("params")

        KgT = sb("KgT", [P, NKMAX], BF16)
        KdT = sb("KdT", [P, 2, NKMAX], BF16)
        Vg = sb("Vg", [P, NKC, 2, VW], BF16)
        Vd = sb("Vd", [P, NKC, 4, VW], BF16)
        KV_b = [Buf("KV%d" % i) for i in range(NKMAX // 256)]

        xt = sb("xt", [P, 2, DC, T], F32); xt_b = [Buf("xt0"), Buf("xt1")]
        ht = sb("ht", [P, 2, DC, T], BF16); ht_b = [Buf("ht0"), Buf("ht1")]
        wA = [sb("wA%d" % i, [P, DC, 256], BF16) for i in range(4)]
        wA_ring = Ring([(wA[i], Buf("wA%d" % i), S.chan("wA%d" % i)) for i in range(4)])
        wO = [sb("wO%d" % i, [P, 2, D], BF16) for i in range(2)]
        wO_ring = Ring([(wO[i], Buf("wO%d" % i), S.chan("wO%d" % i)) for i in range(2)])
        sgt = sb("sgt", [P, 2, T], F32); sg_ring = Ring([(sgt[:, i, :], Buf("sg%d" % i)) for i in range(2)])
        at = sb("at", [P, 2, 2, T], BF16); a_ring = Ring([(at[:, i], Buf("a%d" % i)) for i in range(2)])

        OVW = 14592
        ov = sb("ov", [P, OVW], F32)

        class Carver:
            def __init__(self):
                self.off = 0

            def f32(self, *shape):
                n = int(np.prod(shape))
                ap = ov[:, self.off:self.off + n]
                self.off += n
                assert self.off <= OVW, "overlay overflow %d" % self.off
                if len(shape) > 1:
                    names = " ".join("d%d" % i for i in range(len(shape)))
                    ap = ap.rearrange("p (%s) -> p %s" % (names, names), **{"d%d" % i: shape[i] for i in range(1, len(shape))})
                return ap

            def bf16(self, *shape):
                n = int(np.prod(shape))
                assert n % 2 == 0
                ap = ov[:, self.off:self.off + n // 2].bitcast(BF16)
                self.off += n // 2
                assert self.off <= OVW, "overlay overflow %d" % self.off
                if len(shape) > 1:
                    names = " ".join("d%d" % i for i in range(len(shape)))
                    ap = ap.rearrange("p (%s) -> p %s" % (names, names), **{"d%d" % i: shape[i] for i in range(1, len(shape))})
                return ap

        c1 = Carver()
        p1_tmp = [c1.f32(T) for _ in range(6)]
        p1_tmp_ring = Ring([(p1_tmp[i], Buf("p1tmp%d" % i)) for i in range(6)])
        p1_sq = c1.bf16(2, T); p1_sq_ring = Ring([(p1_sq[:, i, :], Buf("p1sq%d" % i)) for i in range(2)])
        p1_rt = [c1.f32(2, T) for _ in range(2)]
        p1_rt_ring = Ring([(p1_rt[i], Buf("rt%d" % i), S.chan("rt%d" % i)) for i in range(2)])
        p1_hbuf = c1.f32(2, 2, T); p1_hbuf_b = [Buf("hbuf0"), Buf("hbuf1")]
        p1_stg = [c1.f32(2, T) for _ in range(2)]
        p1_stg_ring = Ring([(p1_stg[i], Buf("stg%d" % i), S.chan("stg%d" % i)) for i in range(2)])
        p1_stq = [c1.bf16(2, T) for _ in range(2)]
        p1_stq_ring = Ring([(p1_stq[i], Buf("stq%d" % i), S.chan("stq%d" % i)) for i in range(2)])
        p1_tok = [c1.f32(4, 128) for _ in range(2)]
        p1_tok_ring = Ring([(p1_tok[i], Buf("tok%d" % i), S.chan("tok%d" % i)) for i in range(2)])
        p1_xin = [c1.f32(D) for _ in range(2)]
        p1_xin_ring = Ring([(p1_xin[i], Buf("xin%d" % i), S.chan("xin%d" % i)) for i in range(2)])
        c2 = Carver()
        p2_tmp = [c2.f32(T) for _ in range(4)]
        p2_tmp_ring = Ring([(p2_tmp[i], Buf("p2tmp%d" % i)) for i in range(4)])
        p2_nrm = [(c2.f32(T), Buf("nrm%d" % i)) for i in range(4)]
        p2_sq = c2.bf16(2, T); p2_sq_ring = Ring([(p2_sq[:, i, :], Buf("p2sq%d" % i)) for i in range(2)])
        p2_uph = c2.f32(2, T + 32); p2_uph_b = Buf("uph"); p2_uph_ch = S.chan("uph")
        p2_pa = [c2.f32(2, T + 32) for _ in range(2)]; p2_pa_b = [Buf("pa0"), Buf("pa1")]
        p2_pp = c2.bf16(2, T); p2_pp_b = Buf("pp")
        p2_chh = c2.f32(2, T + 4); p2_chh_b = Buf("chh"); p2_chh_ch = S.chan("chh")
        p2_bt = c2.f32(2, T); p2_bt_b = Buf("bt"); p2_bt_ch = S.chan("bt")
        p2_q = c2.bf16(4, T); p2_q_b = Buf("q"); p2_q_ch = S.chan("q")
        p2_pt = [c2.bf16(2, T) for _ in range(2)]
        p2_pt_ring = Ring([(p2_pt[i], Buf("pt%d" % i)) for i in range(2)])
        p2_y = c2.bf16(8, T); p2_ypc = p2_y; p2_ypc_b = Buf("y"); p2_yh_b = p2_ypc_b
        p2_tok_ring = Ring([(p2_uph.rearrange("p c w -> p (c w)")[:, 0:D], p2_uph_b, S.chan("otok"))])
        c0 = Carver()
        bdw_f = c0.f32(L, 2, 128)
        ov_all_bufs = []

        ps = es.enter_context(nc.psum_tensor("ps", [P, 8, T], F32))
        ps_b = [Buf("ps%d" % i) for i in range(8)]
        psA = Ring([(ps[:, i, :], ps_b[i]) for i in range(4)])
        psB = Ring([(ps[:, i, :], ps_b[i]) for i in (4, 5)])
        psC = Ring([(ps[:, i, :], ps_b[i]) for i in (6, 7)])
        psBC = Ring([(ps[:, i, :], ps_b[i]) for i in (4, 5, 6, 7)])

        ch_misc = S.chan("misc")
        DBG = {}
        ch_dbg = S.chan("dbg")

        def dbg_dump(name, ap, bufs, dt=F32):
            if not DEBUG or name in DBG:
                return
            shp = list(ap.shape)
            t_ = nc.dram_tensor("dbg_" + name, shp, dt, kind="ExternalOutput").ap()
            DBG[name] = t_
            dma("sp", ch_dbg, t_, ap, bufs, ())

        ch_x = [S.chan("x0"), S.chan("x1")]
        ch_xst = [S.chan("xst0"), S.chan("xst1")]
        ch_cache = S.chan("cache")

        def dma(eng, chan, out, in_, reads=(), writes=(), slow=False):
            if slow:
                S.dma(eng, chan, lambda e: e.dma_start(out=out, in_=in_, allow_slow_non_contiguous=True), reads, writes)
            else:
                S.dma(eng, chan, lambda e: e.dma_start(out=out, in_=in_), reads, writes)

        def mm(out, lhsT, rhs, start, stop, reads, writes, inc, tp=None):
            if tp is None:
                S.op("pe", lambda e: e.matmul(out, lhsT=lhsT, rhs=rhs, start=start, stop=stop), reads, writes, inc)
            else:
                S.op("pe", lambda e: e.matmul(out, lhsT=lhsT, rhs=rhs, start=start, stop=stop, tile_position=tp), reads, writes, inc)

        def tr(out, in_, reads, writes, inc):
            S.op("pe", lambda e: e.transpose(out, in_, ident), list(reads) + [cf_b], writes, inc)

        def act(out, in_, func, reads, writes, **kw):
            S.op("act", lambda e: e.activation(out=out, in_=in_, func=func, **kw), reads, writes)

        def tt(out, in0, in1, op, reads, writes, eng="dve"):
            S.op(eng, lambda e: e.tensor_tensor(out=out, in0=in0, in1=in1, op=op), reads, writes)

        def ts(out, in0, s1, s2, op0, op1, reads, writes, eng="dve"):
            if op1 is None:
                S.op(eng, lambda e: e.tensor_scalar(out=out, in0=in0, scalar1=s1, scalar2=None, op0=op0), reads, writes)
            else:
                S.op(eng, lambda e: e.tensor_scalar(out=out, in0=in0, scalar1=s1, scalar2=s2, op0=op0, op1=op1), reads, writes)

        def stt(out, in0, scalar, in1, op0, op1, reads, writes, eng="dve"):
            S.op(eng, lambda e: e.scalar_tensor_tensor(out=out, in0=in0, scalar=scalar, in1=in1, op0=op0, op1=op1), reads, writes)

        def cp(out, in_, reads, writes, eng="dve"):
            S.op(eng, lambda e: e.tensor_copy(out=out, in_=in_), reads, writes)

        def memset(ap, val, writes, eng="dve"):
            S.op(eng, lambda e: e.memset(ap, val), (), writes)

        def rstd_from(ssum_ap, ssum_b, cadd, out_ap, out_b, npart=P):
            ts(out_ap, ssum_ap, float(cadd), None, ALU.add, None, [ssum_b], [out_b])
            act(out_ap, out_ap, AF.Ln, [out_b], [out_b])
            act(out_ap, out_ap, AF.Exp, [out_b], [out_b], scale=-0.5)

        dma("sp", S.chan("cf"), cf[:], consts_in[:, :], (), [cf_b])
        pl = []

        def pdma(out, in_, slow=True, reads=()):
            b_ = Buf("p%d" % len(pl))
            pl.append(b_)
            dma("sp", ch_misc, out, in_, reads, [b_], slow=slow)
        bdwf_b = Buf("bdwf")
        cp(cb[:, 0:256], cf[:, C_ONES:C_ONES + 256], [cf_b], [cb_b])
        memset(Vg[:], 1.0, KV_b)
        memset(Vd[:], 1.0, KV_b)

        for l in range(L):
            for i, src in enumerate((norm_ffn1, norm_mix, norm_ffn2)):
                pdma(ngain[:, l, i, :], src[l].rearrange("(c p) -> p c", p=P))
            pdma(bada[:, l, :], b_ada[l].rearrange("(m p) -> p m", p=P))
            pdma(pscale[:, l, :], pool_scale[l].rearrange("(c p) -> p c", p=P))
            for k in range(3):
                pdma(convw[:, l, k, :], conv_w[l, k].rearrange("(c p) -> p c", p=P))
        pdma(fgain[:], final_norm.rearrange("(c p) -> p c", p=P))
        for hf in range(2):
            pdma(qkn[hf * 64:(hf + 1) * 64, 0, :], gqa_q_norm.rearrange("l d -> d l"))
            pdma(qkn[hf * 64:(hf + 1) * 64, 1, :], gqa_k_norm.rearrange("l d -> d l"))
        pdma(subln[:], diff_subln.rearrange("l d -> d l"))
        pdma(dl[:], diff_lambda.rearrange("(o l) a d -> o l a d", o=1), slow=False)
        for r_ in range(2):
            pdma(condT[:, :, r_], cond[r_].rearrange("(c p) -> p c", p=P))
        memset(bdw_f, 0.0, [bdwf_b])
        for l in range(L):
            for g in range(4):
                c, hf = g // 2, g % 2
                pdma(bdw_f[hf * 64:(hf + 1) * 64, l, c, hf * 64:(hf + 1) * 64], pool_w[l, g], slow=False, reads=[bdwf_b])
        cp(bdw[:], bdw_f, pl + [bdwf_b], [par_b])
        ts(ngain[:], ngain[:], 32.0, None, ALU.mult, None, pl + [par_b], [par_b])
        ts(fgain[:], fgain[:], 32.0, None, ALU.mult, None, [par_b], [par_b])
        ts(qkn[:], qkn[:], 8.0, None, ALU.mult, None, [par_b], [par_b])
        for l in range(L):
            ts(subln[:, l:l + 1], subln[:, l:l + 1], 8.0 * (1.0 - lam_inits[l]), None, ALU.mult, None, [par_b], [par_b])
        for l in range(L):
            tt(dl2[:, l, 0, :], dl[:, l, 0, :], dl[:, l, 1, :], ALU.mult, [par_b], [par_b])
            tt(dl2[:, l, 1, :], dl[:, l, 2, :], dl[:, l, 3, :], ALU.mult, [par_b], [par_b])
        S.op("dve", lambda e: e.tensor_reduce(out=dl3[:], in_=dl2[:].rearrange("o l a d -> o (l a) d"), axis=AX.X, op=ALU.add), [par_b], [par_b])
        act(dl3[:], dl3[:], AF.Exp, [par_b], [par_b])
        for l in range(L):
            tt(dl3[:, 2 * l:2 * l + 1], dl3[:, 2 * l + 1:2 * l + 2], dl3[:, 2 * l:2 * l + 1], ALU.subtract, [par_b], [par_b])
            ts(dl3[:, 2 * l:2 * l + 1], dl3[:, 2 * l:2 * l + 1], -lam_inits[l], None, ALU.add, None, [par_b], [par_b])
        pst, pst_b = psC.next()
        mm(pst[0:64, 0:2 * L], onesf[0:1, 0:64], dl3[0:1, :], True, True, [cf_b, par_b], [pst_b], True)
        cp(nlam[:], pst[0:64, 0:2 * L].rearrange("p (l a) -> p l a", a=2)[:, :, 0], [pst_b], [par_b])
        act(condS[:], condT[:], AF.Silu, [par_b], [par_b])

        for l in range(L):
            pm, pm_b = psC.next()
            pmv = pm[:, 0:144].rearrange("p (m r) -> p m r", r=2)
            for blk in range(36):
                wbuf, wb_b, wch = wA_ring.next()
                dma("pool", wch, wbuf[:, :, :], w_ada[l, :, blk * 256:(blk + 1) * 256].rearrange("(c p) n -> p c n", p=P), (), [wb_b])
                for j in range(2):
                    m = blk * 2 + j
                    for kc in range(DC):
                        mm(pmv[:, m, :], wbuf[:, kc, j * 128:(j + 1) * 128], condS[:, kc, :], kc == 0, kc == DC - 1,
                           [wb_b, par_b], [pm_b], inc=(kc == DC - 1))
            tt(mods[:, l], pmv, bada[:, l, :].unsqueeze(2).to_broadcast([P, 72, 2]), ALU.add, [pm_b, par_b], [mods_b])
            for i in range(3):
                sh = mods[:, l, (3 * i) * 8:(3 * i + 1) * 8, :]
                sc = mods[:, l, (3 * i + 1) * 8:(3 * i + 2) * 8, :]
                gt = mods[:, l, (3 * i + 2) * 8:(3 * i + 3) * 8, :]
                stt(modA[:, l, i], sc, 1.0, ngain[:, l, i, :].unsqueeze(2).to_broadcast([P, DC, 2]), ALU.add, ALU.mult,
                    [mods_b, par_b], [mods_b])
                ts(modG[:, l, i], gt, 0.5 if i != 1 else 1.0, None, ALU.mult, None, [mods_b], [mods_b])

        def modB(l, i, c, r):
            return mods[:, l, 3 * i * 8 + c, r:r + 1]

        class Tile:
            pass

        tiles_s = []
        for i in range(S_S // T):
            t = Tile(); t.gi = i; t.r = 0; t.kind = "s"; t.loc = i; t.tok0 = i * T
            tiles_s.append(t)
        tiles_p = []
        for i in range(NPS // 2):
            t = Tile(); t.gi = S_S // T + i; t.r = 1; t.kind = "p"; t.loc = i; t.tok0 = S_S + i * T
            tiles_p.append(t)

        def load_x_tile(tile, slot, l):
            if l > 0:
                dma("sp", ch_x[slot], xt[:, slot], XS[:, tile.tok0:tile.tok0 + T].rearrange("(c p) t -> p c t", p=P),
                    [XS_b[tile.gi]], [xt_b[slot]])
                return
            src = xs_in if tile.kind == "s" else xp_in
            base = tile.loc * T
            for blk in range(4):
                xin, xin_b, xin_ch = p1_xin_ring.next()
                dma("sp", xin_ch, xin, src[base + blk * 128: base + (blk + 1) * 128, :], (), [xin_b])
                for half in range(2):
                    pt_, pt_b = psB.next()
                    for c4 in range(4):
                        c = half * 4 + c4
                        tr(pt_[:, c4 * 128:(c4 + 1) * 128], xin[:, c * 128:(c + 1) * 128], [xin_b], [pt_b], inc=(c4 == 3))
                    cp(xt[:, slot, half * 4:half * 4 + 4, blk * 128:(blk + 1) * 128],
                       pt_.rearrange("p (c t) -> p c t", t=128), [pt_b], [xt_b[slot]])

        def store_x_tile(tile, slot):
            dma("sp", ch_xst[slot], XS[:, tile.tok0:tile.tok0 + T].rearrange("(c p) t -> p c t", p=P), xt[:, slot],
                [xt_b[slot]], [XS_b[tile.gi]])

        def norm_mod(tile, slot, l, i, tmp_ring, sq_ring):
            ssp, ss_b = psC.next()
            for c in range(DC):
                sq, sq_b = sq_ring.next()
                act(sq, xt[:, slot, c, :], AF.Square, [xt_b[slot]], [sq_b])
                mm(ssp, ones_b, sq, c == 0, c == DC - 1, [cb_b, sq_b], [ss_b], inc=True)
            rs, rs_b = tmp_ring.next()
            dbg_dump("ss0", ssp, [ss_b]) if False else None
            rstd_from(ssp, ss_b, D * EPS, rs, rs_b)
            dbg_dump("rs0", rs, [rs_b])
            t_alt = [tmp_ring.next(), tmp_ring.next()]
            for c in range(DC):
                t1, t1_b = t_alt[c % 2]
                stt(t1, xt[:, slot, c, :], modA[:, l, i, c, tile.r:tile.r + 1], rs, ALU.mult, ALU.mult,
                    [xt_b[slot], mods_b, rs_b], [t1_b])
                ts(ht[:, slot, c, :], t1, modB(l, i, c, tile.r), None, ALU.add, None, [t1_b, mods_b], [ht_b[slot]])

        def ffn(tiles_slots, l, which):
            i_mod = 0 if which == 0 else 2
            w_in = w_ffn_in[which]; w_out = w_ffn_out[which]
            wts = {}

            def weights(jb):
                if jb not in wts:
                    wg, wg_b, wg_ch = wA_ring.next()
                    wu, wu_b, wu_ch = wA_ring.next()
                    wo, wo_b, wo_ch = wO_ring.next()
                    dma("pool", wg_ch, wg[:, :, :], w_in[l, :, jb * 256:(jb + 1) * 256].rearrange("(c p) n -> p c n", p=P), (), [wg_b])
                    dma("pool", wu_ch, wu[:, :, :], w_in[l, :, FF + jb * 256:FF + (jb + 1) * 256].rearrange("(c p) n -> p c n", p=P), (), [wu_b])
                    dma("pool", wo_ch, wo[:, :, :], w_out[l, jb * 256:(jb + 1) * 256, :].rearrange("(j p) n -> p j n", p=P), (), [wo_b])
                    wts[jb] = (wg, wg_b, wu, wu_b, wo, wo_b)
                return wts[jb]

            units = [(jb, tile, slot) for jb in range(FC // 2) for (tile, slot) in tiles_slots]

            def gu(un):
                jb, tile, slot = un
                wg, wg_b, wu, wu_b, wo, wo_b = weights(jb)
                a, a_b = a_ring.next()
                for j in range(2):
                    pg, pg_b = psA.next()
                    pu, pu_b = psA.next()
                    for kc in range(DC):
                        mm(pg, wg[:, kc, j * 128:(j + 1) * 128], ht[:, slot, kc, :], kc == 0, kc == DC - 1,
                           [wg_b, ht_b[slot]], [pg_b], inc=(kc == DC - 1))
                    for kc in range(DC):
                        mm(pu, wu[:, kc, j * 128:(j + 1) * 128], ht[:, slot, kc, :], kc == 0, kc == DC - 1,
                           [wu_b, ht_b[slot]], [pu_b], inc=(kc == DC - 1))
                    sg, sg_b = sg_ring.next()
                    act(sg, pg, AF.Silu, [pg_b], [sg_b])
                    tt(a[:, j, :], sg, pu, ALU.mult, [sg_b, pu_b], [a_b])
                return a, a_b

            def down(un, ares):
                jb, tile, slot = un
                wg, wg_b, wu, wu_b, wo, wo_b = weights(jb)
                a, a_b = ares
                for c in range(DC):
                    po, po_b = psBC.next()
                    for j in range(2):
                        mm(po, wo[:, j, c * 128:(c + 1) * 128], a[:, j, :], j == 0, j == 1, [wo_b, a_b], [po_b], inc=(j == 1))
                    stt(xt[:, slot, c, :], po, modG[:, l, i_mod, c, tile.r:tile.r + 1], xt[:, slot, c, :], ALU.mult, ALU.add,
                        [po_b, mods_b, xt_b[slot]], [xt_b[slot]])

            a_next = gu(units[0])
            for n, un in enumerate(units):
                a_cur = a_next
                if n + 1 < len(units):
                    a_next = gu(units[n + 1])
                down(un, a_cur)

        def key_pos(tile):
            return 256 + tile.loc * T if tile.kind == "s" else tile.loc * T

        def head_norm(pin, pin_b, gain_ap, out_ap, out_b):
            sq, sq_b = p1_sq_ring.next()
            act(sq, pin, AF.Square, [pin_b], [sq_b])
            ssp, ss_b = psC.next()
            mm(ssp, bd64_b, sq, True, True, [cb_b, sq_b], [ss_b], inc=True)
            rs, rs_b = p1_tmp_ring.next()
            rstd_from(ssp, ss_b, 64 * EPS, rs, rs_b)
            stt(out_ap, pin, gain_ap, rs, ALU.mult, ALU.mult, [pin_b, par_b, rs_b], [out_b])

        def rope(x_ap, x_b, rot_lhsT, rt, rt_b, out_ap, out_b):
            pr, pr_b = psC.next()
            mm(pr, rot_lhsT, x_ap, True, True, [cf_b, x_b], [pr_b], inc=True)
            t1, t1_b = p1_tmp_ring.next()
            t2, t2_b = p1_tmp_ring.next()
            tt(t1, x_ap, rt[:, 0, :], ALU.mult, [x_b, rt_b], [t1_b])
            tt(t2, pr, rt[:, 1, :], ALU.mult, [pr_b, rt_b], [t2_b])
            tt(out_ap, t1, t2, ALU.add, [t1_b, t2_b], out_b if isinstance(out_b, list) else [out_b])

        def load_rope(tile, which):
            rt, rt_b, rt_ch = p1_rt_ring.next()
            a, b = ("cg", "sg") if which == "g" else ("cd", "sd")
            t0 = tile.loc * T
            dma("sp", rt_ch, rt[:, 0, :], ropes_in[a][:, t0:t0 + T], (), [rt_b])
            dma("sp", rt_ch, rt[:, 1, :], ropes_in[b][:, t0:t0 + T], (), [rt_b])
            return rt, rt_b

        def to_tokmajor(src_ap, src_b, ncols_used=128):
            ptk, ptk_b = psB.next()
            for blk in range(4):
                tr(ptk[:, blk * 128:(blk + 1) * 128], src_ap[:, blk * 128:(blk + 1) * 128], [src_b], [ptk_b], inc=(blk == 3))
            return ptk.rearrange("p (b f) -> p b f", f=128), ptk_b

        def mix_in(tiles_slots, l):
            order = [0, 1, 3, 2, 4, 5, 6, 7, 8]
            for blk in order:
                wbuf, wb_b, wch = wA_ring.next()
                if blk == 4:
                    for j_ in range(2):
                        for i_ in range(2):
                            h_ = 2 * i_ + j_
                            dma("pool", wch, wbuf[:, :, j_ * 128 + i_ * 64:j_ * 128 + (i_ + 1) * 64],
                                w_mix_in[l, :, 1024 + h_ * 64:1024 + (h_ + 1) * 64].rearrange("(c p) d -> p c d", p=P), (), [wb_b])
                else:
                    dma("pool", wch, wbuf[:, :, :], w_mix_in[l, :, blk * 256:(blk + 1) * 256].rearrange("(c p) n -> p c n", p=P), (), [wb_b])
                for ti, (tile, slot) in enumerate(tiles_slots):
                    is_s = tile.kind == "s"
                    kp = key_pos(tile)
                    kvb = [KV_b[kp // 256], KV_b[kp // 256 + 1]]
                    tsl = slice(tile.tok0, tile.tok0 + T)
                    rt = rt_b = None
                    if is_s and blk in (4, 5):
                        rt, rt_b = load_rope(tile, "g")
                    if is_s and blk in (6, 7):
                        rt, rt_b = load_rope(tile, "d")
                    stage = None
                    if blk in (0, 2, 3):
                        stage = p1_stg_ring.next()
                    if blk in (4, 6):
                        stage = p1_stq_ring.next()
                    for j in range(2):
                        pj, pj_b = psA.next()
                        for kc in range(DC):
                            mm(pj, wbuf[:, kc, j * 128:(j + 1) * 128], ht[:, slot, kc, :], kc == 0, kc == DC - 1,
                               [wb_b, ht_b[slot]], [pj_b], inc=(kc == DC - 1))
                        if blk == 0:
                            act(stage[0][:, j, :], pj, AF.Copy, [pj_b], [stage[1]])
                        elif blk == 1:
                            act(p1_hbuf[:, ti, j, :], pj, AF.Copy, [pj_b], [p1_hbuf_b[ti]])
                        elif blk == 3:
                            tt(stage[0][:, j, :], pj, p1_hbuf[:, ti, j, :], ALU.mult, [pj_b, p1_hbuf_b[ti]], [stage[1]])
                        elif blk == 2:
                            act(stage[0][:, j, :], pj, AF.Copy, [pj_b], [stage[1]])
                        elif blk == 4:
                            if is_s:
                                xn, xn_b = p1_tmp_ring.next()
                                head_norm(pj, pj_b, qkn[:, 0, l:l + 1], xn, xn_b)
                                rope(xn, xn_b, rotg, rt, rt_b, stage[0][:, j, :], stage[1])
                            else:
                                head_norm(pj, pj_b, qkn[:, 0, l:l + 1], stage[0][:, j, :], stage[1])
                        elif blk == 5 and j == 0:
                            xn, xn_b = p1_tmp_ring.next()
                            head_norm(pj, pj_b, qkn[:, 1, l:l + 1], xn, xn_b)
                            if is_s:
                                rope(xn, xn_b, rotg, rt, rt_b, KgT[:, kp:kp + T], kvb)
                            else:
                                cp(KgT[:, kp:kp + T], xn, [xn_b], kvb)
                                tk, tk_b = to_tokmajor(xn, xn_b)
                                st, st_b, st_ch = p1_tok_ring.next()
                                cp(st, tk, [tk_b], [st_b])
                                for sq_ in range(2):
                                    seq = tile.loc * 2 + sq_
                                    for hh in range(2):
                                        dma("sp", st_ch, nkg[seq, l, hh].rearrange("(b p) d -> p b d", p=P),
                                            st[:, sq_ * 2:sq_ * 2 + 2, hh * 64:(hh + 1) * 64], [st_b], ())
                        elif (blk == 5 and j == 1) or blk == 8:
                            vf, vf_b = p1_tmp_ring.next()
                            act(vf, pj, AF.Copy, [pj_b], [vf_b])
                            tk, tk_b = to_tokmajor(vf, vf_b)
                            kc0 = kp // 128
                            if blk == 5:
                                cp(Vg[:, kc0:kc0 + 4, :, 0:64], tk.rearrange("p b (h d) -> p b h d", d=64), [tk_b], kvb)
                            else:
                                cp(Vd[:, kc0:kc0 + 4, 2 * j:2 * j + 2, 0:64], tk.rearrange("p b (h d) -> p b h d", d=64), [tk_b], kvb)
                            if not is_s:
                                st, st_b, st_ch = p1_tok_ring.next()
                                cp(st, tk, [tk_b], [st_b])
                                dst = nvg if blk == 5 else nvd
                                for sq_ in range(2):
                                    seq = tile.loc * 2 + sq_
                                    for hh in range(2):
                                        hidx = hh if blk == 5 else 2 * j + hh
                                        dma("sp", st_ch, dst[seq, l, hidx].rearrange("(b p) d -> p b d", p=P),
                                            st[:, sq_ * 2:sq_ * 2 + 2, hh * 64:(hh + 1) * 64], [st_b], ())
                        elif blk == 6:
                            if is_s:
                                xn, xn_b = p1_tmp_ring.next()
                                act(xn, pj, AF.Copy, [pj_b], [xn_b])
                                rope(xn, xn_b, rotd, rt, rt_b, stage[0][:, j, :], stage[1])
                            else:
                                act(stage[0][:, j, :], pj, AF.Copy, [pj_b], [stage[1]])
                        elif blk == 7:
                            xn, xn_b = p1_tmp_ring.next()
                            act(xn, pj, AF.Copy, [pj_b], [xn_b])
                            if is_s:
                                rope(xn, xn_b, rotd, rt, rt_b, KdT[:, j, kp:kp + T], kvb)
                            else:
                                cp(KdT[:, j, kp:kp + T], xn, [xn_b], kvb)
                                tk, tk_b = to_tokmajor(xn, xn_b)
                                st, st_b, st_ch = p1_tok_ring.next()
                                cp(st, tk, [tk_b], [st_b])
                                for sq_ in range(2):
                                    seq = tile.loc * 2 + sq_
                                    for hh in range(2):
                                        dma("sp", st_ch, nkd[seq, l, 2 * j + hh].rearrange("(b p) d -> p b d", p=P),
                                            st[:, sq_ * 2:sq_ * 2 + 2, hh * 64:(hh + 1) * 64], [st_b], ())
                    if blk == 0:
                        dma("sp", stage[2], UPs[:, tsl].rearrange("(c p) t -> p c t", p=P), stage[0], [stage[1]], [SCR["up"][tile.gi]])
                    elif blk == 3:
                        dma("sp", stage[2], CHs[:, tsl].rearrange("(c p) t -> p c t", p=P), stage[0], [stage[1]], [SCR["ch"][tile.gi]])
                    elif blk == 2:
                        dma("sp", stage[2], Bs[:, tsl].rearrange("(c p) t -> p c t", p=P), stage[0], [stage[1]], [SCR["b"][tile.gi]])
                    elif blk == 4:
                        dma("sp", stage[2], QG[:, tsl].rearrange("(c p) t -> p c t", p=P), stage[0], [stage[1]], [SCR["qg"][tile.gi]])
                    elif blk == 6:
                        dma("sp", stage[2], QD[:, tsl].rearrange("(c p) t -> p c t", p=P), stage[0], [stage[1]], [SCR["qd"][tile.gi]])

        def load_cache(l):
            kvb = [KV_b[0]]
            for (src, nh, dstfn) in ((ckg, 2, lambda c: KgT[:, 0:256]), (ckd, 4, lambda c: KdT[:, c, 0:256])):
                for c in range(nh // 2):
                    for blk in range(2):
                        xin, xin_b, xin_ch = p1_xin_ring.next()
                        for hh in range(2):
                            dma("sp", xin_ch, xin[:, hh * 64:(hh + 1) * 64], src[l, 2 * c + hh, blk * 128:(blk + 1) * 128, :], (), [xin_b])
                        ptk, ptk_b = psB.next()
                        tr(ptk[:, 0:128], xin[:, 0:128], [xin_b], [ptk_b], inc=True)
                        cp(dstfn(c)[:, blk * 128:(blk + 1) * 128], ptk[:, 0:128], [ptk_b], kvb)
            for (src, nh, dst) in ((cvg, 2, Vg), (cvd, 4, Vd)):
                for blk in range(2):
                    xin, xin_b, xin_ch = p1_xin_ring.next()
                    dma("sp", xin_ch, xin[:, 0:nh * 64].rearrange("p (h d) -> p h d", d=64),
                        src[l, :, blk * 128:(blk + 1) * 128, :].rearrange("h p d -> p h d"), (), [xin_b])
                    cp(dst[:, blk, :, 0:64], xin[:, 0:nh * 64].rearrange("p (h d) -> p h d", d=64), [xin_b], kvb)

        def phase1(group_tiles, l):
            for i in range(0, len(group_tiles), 2):
                tsl = [(tl, s_) for s_, tl in enumerate(group_tiles[i:i + 2])]
                for (tl, s_) in tsl:
                    load_x_tile(tl, s_, l)
                dbg_dump("x0", xt[:, 0], [xt_b[0]])
                for (tl, s_) in tsl:
                    norm_mod(tl, s_, l, 0, p1_tmp_ring, p1_sq_ring)
                dbg_dump("h0", ht[:, 0], [ht_b[0]], BF16)
                ffn(tsl, l, 0)
                dbg_dump("x1", xt[:, 0], [xt_b[0]])
                for (tl, s_) in tsl:
                    store_x_tile(tl, s_)
                for (tl, s_) in tsl:
                    norm_mod(tl, s_, l, 1, p1_tmp_ring, p1_sq_ring)
                mix_in(tsl, l)

        def pool_conv(tile, l):
            is_s = tile.kind == "s"
            nseg = 1 if is_s else 2
            SL = T // nseg
            HW = SL + 16
            uph = p2_uph[:, :, 0:nseg * HW].rearrange("p c (s w) -> p c s w", w=HW)
            pa = [p2_pa[i][:, :, 0:nseg * HW].rearrange("p c (s w) -> p c s w", w=HW) for i in range(2)]
            left_edge = (not is_s) or tile.loc == 0
            right_edge = (not is_s) or tile.loc == len(tiles_s) - 1
            for c_ in range(2):
                dma("sp", p2_uph_ch, uph[:, c_, :, 8:8 + SL],
                    UPs[c_ * P:(c_ + 1) * P, tile.tok0:tile.tok0 + T].rearrange("p (s t) -> p s t", s=nseg), [SCR["up"][tile.gi]], [p2_uph_b])
            if left_edge:
                memset(uph[:, :, :, 0:8], 0.0, [p2_uph_b])
            else:
                dma("sp", p2_uph_ch, uph[:, :, 0, 0:8], UPs[:, tile.tok0 - 8:tile.tok0].rearrange("(c p) t -> p c t", p=P),
                    [SCR["up"][tile.gi - 1]], [p2_uph_b], slow=True)
            if right_edge:
                memset(uph[:, :, :, 8 + SL:16 + SL], 0.0, [p2_uph_b])
            else:
                dma("sp", p2_uph_ch, uph[:, :, 0, 8 + SL:16 + SL], UPs[:, tile.tok0 + T:tile.tok0 + T + 8].rearrange("(c p) t -> p c t", p=P),
                    [SCR["up"][tile.gi + 1]], [p2_uph_b], slow=True)
            tt(pa[0][:, :, :, 1:HW], uph[:, :, :, 1:HW], uph[:, :, :, 0:HW - 1], ALU.add, [p2_uph_b], [p2_pa_b[0]])
            pp = p2_pp

            def emit_group(g, a_ap, a_b):
                c, hf = g // 2, g % 2
                sl = slice(hf * 64, (hf + 1) * 64)
                out_ = pp[sl, c, :].rearrange("p (s t) -> p s t", s=nseg)
                stt(out_, a_ap[sl, c, :, 8:8 + SL], cf[sl, C_INVW + c:C_INVW + c + 1], uph[sl, c, :, 8:8 + SL], ALU.mult, ALU.subtract,
                    [a_b, cf_b, p2_uph_b], [p2_pp_b])
                if left_edge:
                    for s_ in range(nseg):
                        t1, t1_b = p2_tmp_ring.next()
                        tt(t1[sl, 0:8], a_ap[sl, c, s_, 8:16], cf[sl, C_ICL + c * 8:C_ICL + c * 8 + 8], ALU.mult, [a_b, cf_b], [t1_b])
                        tt(out_[:, s_, 0:8], t1[sl, 0:8], uph[sl, c, s_, 8:16], ALU.subtract, [t1_b, p2_uph_b], [p2_pp_b])
                if right_edge:
                    for s_ in range(nseg):
                        t1, t1_b = p2_tmp_ring.next()
                        tt(t1[sl, 0:8], a_ap[sl, c, s_, SL:SL + 8], cf[sl, C_ICR + c * 8:C_ICR + c * 8 + 8], ALU.mult, [a_b, cf_b], [t1_b])
                        tt(out_[:, s_, SL - 8:SL], t1[sl, 0:8], uph[sl, c, s_, SL:SL + 8], ALU.subtract, [t1_b, p2_uph_b], [p2_pp_b])

            emit_group(0, pa[0], p2_pa_b[0])
            tt(pa[1][:, :, :, 2:HW - 1], pa[0][:, :, :, 3:HW], pa[0][:, :, :, 1:HW - 2], ALU.add, [p2_pa_b[0]], [p2_pa_b[1]])
            emit_group(1, pa[1], p2_pa_b[1])
            tt(pa[0][:, :, :, 4:HW - 3], pa[1][:, :, :, 2:HW - 5], pa[1][:, :, :, 6:HW - 1], ALU.add, [p2_pa_b[1]], [p2_pa_b[0]])
            emit_group(2, pa[0], p2_pa_b[0])
            tt(pa[1][:, :, :, 8:HW - 7], pa[0][:, :, :, 4:HW - 11], pa[0][:, :, :, 12:HW - 3], ALU.add, [p2_pa_b[0]], [p2_pa_b[1]])
            emit_group(3, pa[1], p2_pa_b[1])
            for c in range(2):
                pq, pq_b = psC.next()
                mm(pq, bdw[:, l, c, :], pp[:, c, :], True, True, [par_b, p2_pp_b], [pq_b], inc=True)
                ts(p2_ypc[:, c, :], pq, pscale[:, l, c:c + 1], None, ALU.mult, None, [pq_b, par_b], [p2_ypc_b])
            CW = SL + 2
            chh = p2_chh[:, :, 0:nseg * CW].rearrange("p c (s w) -> p c s w", w=CW)
            for c_ in range(2):
                dma("sp", p2_chh_ch, chh[:, c_, :, 1:1 + SL],
                    CHs[c_ * P:(c_ + 1) * P, tile.tok0:tile.tok0 + T].rearrange("p (s t) -> p s t", s=nseg), [SCR["ch"][tile.gi]], [p2_chh_b])
            if left_edge:
                memset(chh[:, :, :, 0:1], 0.0, [p2_chh_b])
            else:
                dma("sp", p2_chh_ch, chh[:, :, 0, 0:1], CHs[:, tile.tok0 - 1:tile.tok0].rearrange("(c p) t -> p c t", p=P),
                    [SCR["ch"][tile.gi - 1]], [p2_chh_b], slow=True)
            if right_edge:
                memset(chh[:, :, :, SL + 1:SL + 2], 0.0, [p2_chh_b])
            else:
                dma("sp", p2_chh_ch, chh[:, :, 0, SL + 1:SL + 2], CHs[:, tile.tok0 + T:tile.tok0 + T + 1].rearrange("(c p) t -> p c t", p=P),
                    [SCR["ch"][tile.gi + 1]], [p2_chh_b], slow=True)
            dma("sp", p2_bt_ch, p2_bt, Bs[:, tile.tok0:tile.tok0 + T].rearrange("(c p) t -> p c t", p=P), [SCR["b"][tile.gi]], [p2_bt_b])
            for c in range(2):
                acc, acc_b = p2_tmp_ring.next()
                accv = acc.rearrange("p (s t) -> p s t", s=nseg)
                ts(accv, chh[:, c, :, 0:SL], convw[:, l, 0, c:c + 1], None, ALU.mult, None, [p2_chh_b, par_b], [acc_b])
                stt(accv, chh[:, c, :, 1:1 + SL], convw[:, l, 1, c:c + 1], accv, ALU.mult, ALU.add, [p2_chh_b, par_b, acc_b], [acc_b])
                stt(accv, chh[:, c, :, 2:2 + SL], convw[:, l, 2, c:c + 1], accv, ALU.mult, ALU.add, [p2_chh_b, par_b, acc_b], [acc_b])
                tt(p2_ypc[:, 2 + c, :], acc, p2_bt[:, c, :], ALU.mult, [acc_b, p2_bt_b], [p2_ypc_b])

        def attention(tile, l):
            is_s = tile.kind == "s"
            dma("sp", p2_q_ch, p2_q[:, 0:2, :], QG[:, tile.tok0:tile.tok0 + T].rearrange("(c p) t -> p c t", p=P), [SCR["qg"][tile.gi]], [p2_q_b])
            dma("sp", p2_q_ch, p2_q[:, 2:4, :], QD[:, tile.tok0:tile.tok0 + T].rearrange("(c p) t -> p c t", p=P), [SCR["qd"][tile.gi]], [p2_q_b])
            if is_s:
                segs = [(0, T, list(range(NK_S // 128)))]
            else:
                segs = [(0, 256, [tile.loc * 4, tile.loc * 4 + 1]), (256, 256, [tile.loc * 4 + 2, tile.loc * 4 + 3])]
            kv_all = KV_b
            steps = []
            for pi, (kind, idx) in enumerate([("g", 0), ("g", 1), ("d", 0), ("d", 1), ("d", 2), ("d", 3)]):
                units = []
                if kind == "g":
                    for hf in range(2):
                        units.append(dict(qrow=hf * 64, dk=64, qch=idx, kap=lambda k0, hf=hf: KgT[hf * 64:(hf + 1) * 64, k0:k0 + 128],
                                          vap=lambda kc, hf=hf: Vg[:, kc, hf, 0:65], head=idx + 2 * hf))
                    scale = 64 ** -0.5
                else:
                    h = idx
                    for comp in range(2):
                        rb = (h % 2) * 64 + comp * 32
                        units.append(dict(qrow=rb, dk=32, qch=2 + h // 2, kap=lambda k0, rb=rb, h=h: KdT[rb:rb + 32, h // 2, k0:k0 + 128],
                                          vap=lambda kc, h=h: Vd[:, kc, h, 0:65], head=h))
                    scale = 32 ** -0.5
                pc = dict(units=units, scale=scale, kind=kind, idx=idx, o=None, oring=psB,
                          nrm=p2_nrm[(pi % 2) * 2:(pi % 2) * 2 + 2])
                plist = []
                for (q0, qn, kcs) in segs:
                    for ki, kc in enumerate(kcs):
                        plist.append(dict(pc=pc, q0=q0, qn=qn, ki=ki, nk=len(kcs), kc=kc, last=False))
                plist[-1]["last"] = True
                steps += plist

            def qk(st):
                pc = st["pc"]; q0, qn, kc = st["q0"], st["qn"], st["kc"]
                if psA.i % 2:
                    psA.i += 1
                s_aps = [psA.next(), psA.next()]
                bank0 = (psA.i - 2) % 4
                if KEEP_WARM:
                    for _ in range(KEEP_WARM):
                        mm(s_aps[0][0][:, :], ones_b, p2_q[:, 0, :], True, True, [cb_b, p2_q_b], [s_aps[0][1]], inc=False)
                for ui, u in enumerate(pc["units"]):
                    rb, dk = u["qrow"], u["dk"]
                    mm(s_aps[ui][0][:, 0:qn], u["kap"](kc * 128), p2_q[rb:rb + dk, u["qch"], q0:q0 + qn], True, True,
                       kv_all + [p2_q_b], [s_aps[ui][1]], inc=(ui == 1), tp=(rb, 0))
                return s_aps, bank0

            def expv(st, sres):
                pc = st["pc"]; q0, qn, kc, ki, nk = st["q0"], st["qn"], st["kc"], st["ki"], st["nk"]
                s_aps, bank0 = sres
                if pc["o"] is None:
                    pc["o"] = [pc["oring"].next(), pc["oring"].next()]
                pt_, pt_b = p2_pt_ring.next()
                act(pt_[:, :, 0:qn], ps[:, bank0:bank0 + 2, 0:qn], AF.Exp, [s_aps[0][1], s_aps[1][1]], [pt_b], scale=pc["scale"])
                for ui, u in enumerate(pc["units"]):
                    mm(pc["o"][ui][0][0:65, q0:q0 + qn], u["vap"](kc), pt_[:, ui, 0:qn], ki == 0, ki == nk - 1,
                       kv_all + [pt_b], [pc["o"][ui][1]], inc=(ki == nk - 1))

            def norm1(pc):
                for ui in range(2):
                    o_ap, o_b = pc["o"][ui]
                    U, U_b = pc["nrm"][ui]
                    cp(U[64:65, :], o_ap[64:65, :], [o_b], [U_b])

            def norm2(pc):
                for ui, u in enumerate(pc["units"]):
                    o_ap, o_b = pc["o"][ui]
                    U, U_b = pc["nrm"][ui]
                    S.op("dve", lambda e, U=U: e.reciprocal(out=U[64:65, :], in_=U[64:65, :]), [U_b], [U_b])
                    zh, zh_b = p2_sq_ring.next()
                    zl, zl_b = p2_sq_ring.next()
                    cp(zh[64:65, :], U[64:65, :], [U_b], [zh_b])
                    tt(zl[64:65, :], U[64:65, :], zh[64:65, :], ALU.subtract, [U_b, zh_b], [zl_b])
                    pz, pz_b = psC.next()
                    mm(pz[0:64, :], ones_b[64:65, 0:64], zh[64:65, :], True, False, [cb_b, zh_b], [pz_b], inc=False)
                    mm(pz[0:64, :], ones_b[64:65, 0:64], zl[64:65, :], False, True, [cb_b, zl_b], [pz_b], inc=True)
                    cp(U[0:64, :], pz[0:64, :], [pz_b], [U_b])
                    if pc["kind"] == "g":
                        hd = u["head"]
                        tt(p2_y[(hd % 2) * 64:(hd % 2) * 64 + 64, 4 + hd // 2, :], o_ap[0:64, :], U[0:64, :], ALU.mult, [o_b, U_b], [p2_yh_b])
                    else:
                        tt(U[0:64, :], o_ap[0:64, :], U[0:64, :], ALU.mult, [o_b, U_b], [U_b])
                if pc["kind"] == "d":
                    (U1, U1_b), (U2, U2_b) = pc["nrm"]
                    stt(U1[0:64, :], U2[0:64, :], nlam[:, l:l + 1], U1[0:64, :], ALU.mult, ALU.add, [U2_b, par_b, U1_b], [U1_b])
                    sq, sq_b = p2_sq_ring.next()
                    tt(sq[0:64, :], U1[0:64, :], U1[0:64, :], ALU.mult, [U1_b], [sq_b])
                    pc["sq"] = (sq, sq_b)

            def norm3(pc):
                if pc["kind"] != "d":
                    return
                h = pc["idx"]
                (U1, U1_b), (U2, U2_b) = pc["nrm"]
                sq, sq_b = pc["sq"]
                pss, pss_b = psC.next()
                mm(pss[0:64, :], ones_b[0:64, 0:64], sq[0:64, :], True, True, [cb_b, sq_b], [pss_b], inc=True)
                rstd_from(pss[0:64, :], pss_b, 64 * EPS, U2[0:64, :], U2_b)
                stt(p2_y[(h % 2) * 64:(h % 2) * 64 + 64, 6 + h // 2, :], U1[0:64, :], subln[:, l:l + 1], U2[0:64, :], ALU.mult, ALU.mult,
                    [U1_b, par_b, U2_b], [p2_yh_b])

            pending = []
            s_next = qk(steps[0])
            for n, st in enumerate(steps):
                s_cur = s_next
                if n + 1 < len(steps):
                    s_next = qk(steps[n + 1])
                expv(st, s_cur)
                ready = [p_ for p_ in pending if p_[0] <= 0]
                pending = [[p_[0] - 1, p_[1]] for p_ in pending if p_[0] > 0]
                for p_ in ready:
                    p_[1]()
                if st["last"]:
                    pc = st["pc"]
                    norm1(pc)
                    norm2(pc)
                    pending.append([0, lambda pc=pc: norm3(pc)])
            while pending:
                ready = [p_ for p_ in pending if p_[0] <= 0]
                pending = [[p_[0] - 1, p_[1]] for p_ in pending if p_[0] > 0]
                for p_ in ready:
                    p_[1]()

        def mix_out(tiles_done, l):
            raise NotImplementedError

        def mix_out_tile(tile, slot, l):
            for blk in range(4):
                wo, wo_b, wo_ch = wO_ring.next()
                dma("pool", wo_ch, wo[:, :, :], w_mix_out[l, blk * 256:(blk + 1) * 256, :].rearrange("(j p) n -> p j n", p=P), (), [wo_b])
                for c in range(DC):
                    po, po_b = psC.next()
                    for j in range(2):
                        mm(po, wo[:, j, c * 128:(c + 1) * 128], p2_y[:, blk * 2 + j, :], j == 0, j == 1,
                           [wo_b, p2_ypc_b], [po_b], inc=(j == 1))
                    stt(xt[:, slot, c, :], po, modG[:, l, 1, c, tile.r:tile.r + 1], xt[:, slot, c, :], ALU.mult, ALU.add,
                        [po_b, mods_b, xt_b[slot]], [xt_b[slot]])

        def final_out(tile, slot):
            ssp, ss_b = psC.next()
            for c in range(DC):
                sq, sq_b = p2_sq_ring.next()
                act(sq, xt[:, slot, c, :], AF.Square, [xt_b[slot]], [sq_b])
                mm(ssp, ones_b, sq, c == 0, c == DC - 1, [cb_b, sq_b], [ss_b], inc=True)
            rs, rs_b = p2_tmp_ring.next()
            rstd_from(ssp, ss_b, D * EPS, rs, rs_b)
            for c in range(DC):
                stt(xt[:, slot, c, :], xt[:, slot, c, :], fgain[:, c:c + 1], rs, ALU.mult, ALU.mult,
                    [xt_b[slot], par_b, rs_b], [xt_b[slot]])
            dst = ys_out if tile.kind == "s" else yp_out
            base = tile.loc * T
            for blk in range(4):
                st, st_b, st_ch = p2_tok_ring.next()
                for half in range(2):
                    pt_, pt_b = psA.next()
                    for c4 in range(4):
                        c = half * 4 + c4
                        tr(pt_[:, c4 * 128:(c4 + 1) * 128], xt[:, slot, c, blk * 128:(blk + 1) * 128], [xt_b[slot]], [pt_b], inc=(c4 == 3))
                    cp(st[:, half * 512:(half + 1) * 512], pt_, [pt_b], [st_b])
                dma("sp", st_ch, dst[base + blk * 128:base + (blk + 1) * 128, :], st, [st_b], ())
                out_chans.add(st_ch)

        out_chans = set()

        def phase2(group_tiles, l):
            for i in range(0, len(group_tiles), 2):
                tsl = [(tl, s_) for s_, tl in enumerate(group_tiles[i:i + 2])]
                for (tl, s_) in tsl:
                    load_x_tile(tl, s_, 1)
                    pool_conv(tl, l)
                    attention(tl, l)
                    mix_out_tile(tl, s_, l)
                for (tl, s_) in tsl:
                    norm_mod(tl, s_, l, 2, p2_tmp_ring, p2_sq_ring)
                ffn(tsl, l, 1)
                for (tl, s_) in tsl:
                    if l == L - 1:
                        final_out(tl, s_)
                    else:
                        store_x_tile(tl, s_)

        for (grp, tiles) in (("p", tiles_p), ("s", tiles_s)):
            for l in range(L):
                S.barrier()
                if grp == "s":
                    load_cache(l)
                phase1(tiles, l)
                S.barrier()
                phase2(tiles, l)
        dbg_dump("mods", mods[:], [mods_b])
        dbg_dump("modA", modA[:], [mods_b])
        S.barrier()

        sems = {k: es.enter_context(nc.semaphore(k)) for k in S.sem_keys()}
        with nc.Block() as block:
            S.emit(block, sems)
    return nc, S


_W_NAMES = ["w_ada", "b_ada", "norm_ffn1", "norm_mix", "norm_ffn2", "w_ffn1_in", "w_ffn1_out", "w_ffn2_in",
            "w_ffn2_out", "w_mix_in", "pool_w", "pool_scale", "conv_w", "gqa_q_norm", "gqa_k_norm", "diff_lambda",
            "diff_subln", "w_mix_out", "final_norm"]


def run_cores(inputs, n_cores, trace=False):
    f = lambda a: np.ascontiguousarray(np.asarray(a, dtype=np.float32))
    x_prompt = f(inputs["x_prompt"]); x_sample = f(inputs["x_sample"])
    B, SEQ, _ = x_prompt.shape
    DB, S_S, _ = x_sample.shape
    L = inputs["w_ada"].shape[0]
    assert DB == n_cores and B % n_cores == 0 and SEQ == 256
    NPS = B // n_cores
    nc, S = build_program(L, S_S, NPS)
    cg, sg = _rope_tables(S_S, 64, 2)
    cd, sd = _rope_tables(S_S, 32, 4)
    shared = {k: f(inputs[k]) for k in _W_NAMES}
    shared["consts"] = _const_pack(256 if False else 256)
    shared["rope_cg"] = cg; shared["rope_sg"] = sg; shared["rope_cd"] = cd; shared["rope_sd"] = sd
    ckg = f(inputs["cache_gqa_k"]); cvg = f(inputs["cache_gqa_v"]); ckd = f(inputs["cache_diff_k"]); cvd = f(inputs["cache_diff_v"])
    c = f(inputs["c"]); c_ctx = f(inputs["c_ctx"])
    in_maps = []
    for b in range(n_cores):
        m = dict(shared)
        m["xs"] = x_sample[b]
        m["xp"] = np.ascontiguousarray(x_prompt[b * NPS:(b + 1) * NPS].reshape(NPS * 256, D))
        m["ckg"] = ckg[b]; m["cvg"] = cvg[b]; m["ckd"] = ckd[b]; m["cvd"] = cvd[b]
        m["cond"] = np.ascontiguousarray(np.stack([c[b], c_ctx], axis=0))
        in_maps.append(m)
    res = run_bass_kernel_spmd(nc, in_maps, core_ids=list(range(n_cores)), trace=trace)
    rs = res.results
    global LAST_RESULTS
    LAST_RESULTS = rs
    y_prompt = np.concatenate([r["yp"].reshape(NPS, 256, D) for r in rs], axis=0)
    y_sample = np.stack([r["ys"] for r in rs], axis=0)
    outs = [y_prompt, y_sample]
    for k in ("nkg", "nvg", "nkd", "nvd"):
        outs.append(np.concatenate([r[k] for r in rs], axis=0))
    return tuple(np.ascontiguousarray(o, dtype=np.float32) for o in outs), res


LAST_RESULTS = None


def kernel(**inputs):
    outs, _ = run_cores(inputs, 8)
    return outs
```

```python
import math
from contextlib import ExitStack

import numpy as np
import concourse.bass as bass
import concourse.mybir as mybir
from concourse.bass_utils import run_bass_kernel_spmd

F32 = mybir.dt.float32
BF16 = mybir.dt.bfloat16
AF = mybir.ActivationFunctionType
ALU = mybir.AluOpType
AX = mybir.AxisListType

ENGS = ("pe", "act", "dve", "pool", "sp")
DEBUG = False
KEEP_WARM = 1
P = 128
D = 1024
DC = 8
T = 512
FF = 2816
FC = 22
EPS = 1e-6
N_MOD = 9


class Buf:
    __slots__ = ("name", "w", "r")

    def __init__(self, name=""):
        self.name = name
        self.w = None
        self.r = {}


class Chan:
    __slots__ = ("key", "count")

    def __init__(self, key):
        self.key = key
        self.count = 0


class Sched:
    def __init__(self):
        self.streams = {e: [] for e in ENGS}
        self.cnt = {e: 0 for e in ENGS}
        self.seen = {e: {} for e in ENGS}
        self.chans = []
        self.chan_by_key = {}
        self.inter_waits = {}
        self.n_instr = 0

    def chan(self, name):
        c = Chan("ch%d_%s" % (len(self.chans), name))
        self.chans.append(c)
        self.chan_by_key[c.key] = c
        return c

    def _need(self, eng, tok, waits):
        if tok is None:
            return
        k, v = tok
        if eng == "pe" and k == "e_pe":
            return
        if self.seen[eng].get(k, 0) >= v:
            return
        if waits.get(k, 0) < v:
            waits[k] = v

    def _deps(self, eng, reads, writes):
        waits = {}
        for b in reads:
            self._need(eng, b.w, waits)
        for b in writes:
            self._need(eng, b.w, waits)
            for k, v in b.r.items():
                self._need(eng, (k, v), waits)
        for k, v in waits.items():
            self.seen[eng][k] = v
            self.streams[eng].append(("wait", k, v))
            c = self.chan_by_key.get(k)
            if c is not None and c.count > v:
                self.inter_waits[k] = self.inter_waits.get(k, 0) + 1

    @staticmethod
    def _commit(tok, reads, writes):
        k, v = tok
        for b in reads:
            if b.r.get(k, 0) < v:
                b.r[k] = v
        for b in writes:
            b.w = tok
            b.r = {}

    def op(self, eng, fn, reads=(), writes=(), inc=True):
        self.n_instr += 1
        self._deps(eng, reads, writes)
        if inc:
            self.cnt[eng] += 1
            tok = ("e_" + eng, self.cnt[eng])
            self.streams[eng].append(("op", fn, tok))
        else:
            self.streams[eng].append(("op", fn, None))
            tok = ("e_" + eng, self.cnt[eng] + 1)
        self._commit(tok, reads, writes)

    def dma(self, eng, chan, fn, reads=(), writes=()):
        self.n_instr += 1
        self._deps(eng, reads, writes)
        chan.count += 16
        tok = (chan.key, chan.count)
        self.streams[eng].append(("dma", fn, tok))
        self._commit(tok, reads, writes)

    def wait_all(self, eng, bufs):
        self._deps(eng, (), bufs)

    def barrier(self):
        toks = [("e_" + e, self.cnt[e]) for e in ENGS if self.cnt[e] > 0]
        toks += [(c.key, c.count) for c in self.chans if c.count > 0]
        for e in ENGS:
            for k, v in toks:
                if e == "pe" and k == "e_pe":
                    continue
                if self.seen[e].get(k, 0) < v:
                    self.seen[e][k] = v
                    self.streams[e].append(("wait", k, v))

    def sem_keys(self):
        keys = ["e_" + e for e in ENGS if self.cnt[e] > 0]
        keys += [c.key for c in self.chans if c.count > 0]
        return keys

    def emit(self, block, sems):
        names = {"pe": "tensor", "act": "scalar", "dve": "vector", "pool": "gpsimd", "sp": "sync"}
        for ename in ENGS:
            stream = self.streams[ename]
            if not stream:
                continue

            def _f(eobj, stream=stream):
                for item in stream:
                    if item[0] == "wait":
                        eobj.wait_ge(sems[item[1]], item[2])
                    else:
                        ins = item[1](eobj)
                        if item[2] is not None:
                            ins.then_inc(sems[item[2][0]], 16 if item[0] == "dma" else 1)
            getattr(block, names[ename])(_f)


class Ring:
    def __init__(self, slots):
        self.slots = slots
        self.i = 0

    def next(self):
        s = self.slots[self.i % len(self.slots)]
        self.i += 1
        return s


def _rope_tables(seq, dim, reps):
    grid_w = 64
    rows = seq // grid_w
    row = np.broadcast_to(np.arange(rows, dtype=np.float32)[:, None], (rows, grid_w)).reshape(seq)
    col = np.broadcast_to(np.arange(grid_w, dtype=np.float32)[None, :], (rows, grid_w)).reshape(seq)
    quarter = dim // 4
    inv = (1.0 / (np.float32(10000.0) ** (np.arange(quarter, dtype=np.float32) / np.float32(quarter)))).astype(np.float32)
    ar = row[:, None] * inv
    ac = col[:, None] * inv
    ang = np.concatenate([ar, ar, ac, ac], axis=-1).astype(np.float32)
    cos = np.cos(ang).astype(np.float32).T
    sin = np.sin(ang).astype(np.float32).T
    return np.ascontiguousarray(np.tile(cos, (reps, 1))), np.ascontiguousarray(np.tile(sin, (reps, 1)))


def _rot_lhsT(dim):
    q = dim // 4
    R = np.zeros((dim, dim), np.float32)
    for i in range(q):
        R[i, i + q] = -1.0
        R[i + q, i] = 1.0
        R[i + 2 * q, i + 3 * q] = -1.0
        R[i + 3 * q, i + 2 * q] = 1.0
    full = np.zeros((128, 128), np.float32)
    for b in range(128 // dim):
        full[b * dim:(b + 1) * dim, b * dim:(b + 1) * dim] = R
    return np.ascontiguousarray(full.T)


def _const_pack(seq_len_for_pool):
    ident = np.eye(128, dtype=np.float32)
    ones = np.ones((128, 128), np.float32)
    bd64 = np.zeros((128, 128), np.float32)
    bd64[:64, :64] = 1.0
    bd64[64:, 64:] = 1.0
    rotg = _rot_lhsT(64)
    rotd = _rot_lhsT(32)
    wins = (2, 4, 8, 16)
    invw = np.zeros((128, 2), np.float32)
    icl = np.zeros((128, 2, 8), np.float32)
    icr = np.zeros((128, 2, 8), np.float32)
    s = seq_len_for_pool
    for c in range(2):
        for hf in range(2):
            w = wins[2 * c + hf]
            sl = slice(hf * 64, (hf + 1) * 64)
            invw[sl, c] = 1.0 / w
            for i in range(8):
                t = i
                lo = max(t - w // 2, 0); hi = min(t + w // 2, s)
                icl[sl, c, i] = 1.0 / (hi - lo)
                t = s - 8 + i
                lo = max(t - w // 2, 0); hi = min(t + w // 2, s)
                icr[sl, c, i] = 1.0 / (hi - lo)
    return np.ascontiguousarray(np.concatenate(
        [ident, ones, bd64, rotg, rotd, invw, icl.reshape(128, 16), icr.reshape(128, 16)], axis=1))


C_IDENT, C_ONES, C_BD64, C_ROTG, C_ROTD = 0, 128, 256, 384, 512
C_INVW, C_ICL, C_ICR, NCF = 640, 642, 658, 674


def build_program(L, S_S, NPS):
    assert S_S % T == 0 and NPS % 2 == 0
    NTOK_P = NPS * 256
    NTOK = S_S + NTOK_P
    NK_S = 256 + S_S
    NKMAX = max(NK_S, NTOK_P)
    NKC = NKMAX // 128
    VW = 66
    lam_inits = [0.8 - 0.6 * math.exp(-0.3 * l) for l in range(L)]

    nc = bass.Bass("TRN2", target_bir_lowering=False)
    S = Sched()

    def din(name, shape, dt=F32):
        return nc.dram_tensor(name, list(shape), dt, kind="ExternalInput").ap()

    def dout(name, shape):
        return nc.dram_tensor(name, list(shape), F32, kind="ExternalOutput").ap()

    def dscr(name, shape, dt):
        return nc.dram_tensor(name, list(shape), dt, kind="Internal").ap()

    xs_in = din("xs", [S_S, D]); xp_in = din("xp", [NTOK_P, D])
    ckg = din("ckg", [L, 2, 256, 64]); cvg = din("cvg", [L, 2, 256, 64])
    ckd = din("ckd", [L, 4, 256, 64]); cvd = din("cvd", [L, 4, 256, 64])
    cond = din("cond", [2, D])
    w_ada = din("w_ada", [L, D, N_MOD * D]); b_ada = din("b_ada", [L, N_MOD * D])
    norm_ffn1 = din("norm_ffn1", [L, D]); norm_mix = din("norm_mix", [L, D]); norm_ffn2 = din("norm_ffn2", [L, D])
    w_ffn_in = [din("w_ffn1_in", [L, D, 2 * FF]), din("w_ffn2_in", [L, D, 2 * FF])]
    w_ffn_out = [din("w_ffn1_out", [L, FF, D]), din("w_ffn2_out", [L, FF, D])]
    w_mix_in = din("w_mix_in", [L, D, 2304])
    pool_w = din("pool_w", [L, 4, 64, 64]); pool_scale = din("pool_scale", [L, 256]); conv_w = din("conv_w", [L, 3, 256])
    gqa_q_norm = din("gqa_q_norm", [L, 64]); gqa_k_norm = din("gqa_k_norm", [L, 64])
    diff_lambda = din("diff_lambda", [L, 4, 32]); diff_subln = din("diff_subln", [L, 64])
    w_mix_out = din("w_mix_out", [L, D, D]); final_norm = din("final_norm", [D])
    consts_in = din("consts", [P, NCF])
    ropes_in = {k: din("rope_" + k, [P, S_S]) for k in ("cg", "sg", "cd", "sd")}

    ys_out = dout("ys", [S_S, D]); yp_out = dout("yp", [NTOK_P, D])
    nkg = dout("nkg", [NPS, L, 2, 256, 64]); nvg = dout("nvg", [NPS, L, 2, 256, 64])
    nkd = dout("nkd", [NPS, L, 4, 256, 64]); nvd = dout("nvd", [NPS, L, 4, 256, 64])

    XS = dscr("XS", [D, NTOK], F32)
    QG = dscr("QG", [256, NTOK], BF16); QD = dscr("QD", [256, NTOK], BF16)
    UPs = dscr("UPs", [256, NTOK], F32); CHs = dscr("CHs", [256, NTOK], F32); Bs = dscr("Bs", [256, NTOK], F32)
    n_tiles = NTOK // T
    XS_b = [Buf("XS%d" % i) for i in range(n_tiles)]
    SCR = {k: [Buf("SCR%s%d" % (k, i)) for i in range(n_tiles)] for k in ("up", "ch", "b", "qg", "qd")}

    with ExitStack() as es:
        def sb(name, shape, dt):
            return es.enter_context(nc.sbuf_tensor(name, list(shape), dt))

        cf = sb("cf", [P, NCF], F32); cf_b = Buf("cf")
        cb = sb("cb", [P, 384], BF16); cb_b = Buf("cb")
        ident = cf[:, C_IDENT:C_IDENT + 128]
        ones_b = cb[:, 0:128]; bd64_b = cb[:, 128:256]
        rotg = cf[:, C_ROTG:C_ROTG + 128]; rotd = cf[:, C_ROTD:C_ROTD + 128]
        onesf = cf[:, C_ONES:C_ONES + 128]
        mods = sb("mods", [P, L, 72, 2], F32); mods_b = Buf("mods")
        bada = sb("bada", [P, L, 72], F32)
        ngain = sb("ngain", [P, L, 3, DC], F32)
        fgain = sb("fgain", [P, DC], F32)
        modA = sb("modA", [P, L, 3, DC, 2], F32)
        modG = sb("modG", [P, L, 3, DC, 2], F32)
        qkn = sb("qkn", [P, 2, L], F32)
        subln = sb("subln", [64, L], F32)
        pscale = sb("pscale", [P, L, 2], F32)
        convw = sb("convw", [P, L, 3, 2], F32)
        bdw = sb("bdw", [P, L, 2, 128], BF16)
        nlam = sb("nlam", [64, L], F32)
        dl = sb("dl", [1, L, 4, 32], F32)
        dl2 = sb("dl2", [1, L, 2, 32], F32)
        dl3 = sb("dl3", [1, L * 2], F32)
        condT = sb("condT", [P, DC, 2], F32)
        condS = sb("condS", [P, DC, 2], BF16)
        par_b = Buf("params")

        KgT = sb("KgT", [P, NKMAX], BF16)
        KdT = sb("KdT", [P, 2, NKMAX], BF16)
        Vg = sb("Vg", [P, NKC, 2, VW], BF16)
        Vd = sb("Vd", [P, NKC, 4, VW], BF16)
        KV_b = [Buf("KV%d" % i) for i in range(NKMAX // 256)]

        xt = sb("xt", [P, 2, DC, T], F32); xt_b = [Buf("xt0"), Buf("xt1")]
        ht = sb("ht", [P, 2, DC, T], BF16); ht_b = [Buf("ht0"), Buf("ht1")]
        wA = [sb("wA%d" % i, [P, DC, 256], BF16) for i in range(4)]
        wA_ring = Ring([(wA[i], Buf("wA%d" % i), S.chan("wA%d" % i)) for i in range(4)])
        wO = [sb("wO%d" % i, [P, 2, D], BF16) for i in range(2)]
        wO_ring = Ring([(wO[i], Buf("wO%d" % i), S.chan("wO%d" % i)) for i in range(2)])
        sgt = sb("sgt", [P, 2, T], F32); sg_ring = Ring([(sgt[:, i, :], Buf("sg%d" % i)) for i in range(2)])
        at = sb("at", [P, 2, 2, T], BF16); a_ring = Ring([(at[:, i], Buf("a%d" % i)) for i in range(2)])

        OVW = 14592
        ov = sb("ov", [P, OVW], F32)

        class Carver:
            def __init__(self):
                self.off = 0

            def f32(self, *shape):
                n = int(np.prod(shape))
                ap = ov[:, self.off:self.off + n]
                self.off += n
                assert self.off <= OVW, "overlay overflow %d" % self.off
                if len(shape) > 1:
                    names = " ".join("d%d" % i for i in range(len(shape)))
                    ap = ap.rearrange("p (%s) -> p %s" % (names, names), **{"d%d" % i: shape[i] for i in range(1, len(shape))})
                return ap

            def bf16(self, *shape):
                n = int(np.prod(shape))
                assert n % 2 == 0
                ap = ov[:, self.off:self.off + n // 2].bitcast(BF16)
                self.off += n // 2
                assert self.off <= OVW, "overlay overflow %d" % self.off
                if len(shape) > 1:
                    names = " ".join("d%d" % i for i in range(len(shape)))
                    ap = ap.rearrange("p (%s) -> p %s" % (names, names), **{"d%d" % i: shape[i] for i in range(1, len(shape))})
                return ap

        c1 = Carver()
        p1_tmp = [c1.f32(T) for _ in range(6)]
        p1_tmp_ring = Ring([(p1_tmp[i], Buf("p1tmp%d" % i)) for i in range(6)])
        p1_sq = c1.bf16(2, T); p1_sq_ring = Ring([(p1_sq[:, i, :], Buf("p1sq%d" % i)) for i in range(2)])
        p1_rt = [c1.f32(2, T) for _ in range(2)]
        p1_rt_ring = Ring([(p1_rt[i], Buf("rt%d" % i), S.chan("rt%d" % i)) for i in range(2)])
        p1_hbuf = c1.f32(2, 2, T); p1_hbuf_b = [Buf("hbuf0"), Buf("hbuf1")]
        p1_stg = [c1.f32(2, T) for _ in range(2)]
        p1_stg_ring = Ring([(p1_stg[i], Buf("stg%d" % i), S.chan("stg%d" % i)) for i in range(2)])
        p1_stq = [c1.bf16(2, T) for _ in range(2)]
        p1_stq_ring = Ring([(p1_stq[i], Buf("stq%d" % i), S.chan("stq%d" % i)) for i in range(2)])
        p1_tok = [c1.f32(4, 128) for _ in range(2)]
        p1_tok_ring = Ring([(p1_tok[i], Buf("tok%d" % i), S.chan("tok%d" % i)) for i in range(2)])
        p1_xin = [c1.f32(D) for _ in range(2)]
        p1_xin_ring = Ring([(p1_xin[i], Buf("xin%d" % i), S.chan("xin%d" % i)) for i in range(2)])
        c2 = Carver()
        p2_tmp = [c2.f32(T) for _ in range(4)]
        p2_tmp_ring = Ring([(p2_tmp[i], Buf("p2tmp%d" % i)) for i in range(4)])
        p2_nrm = [(c2.f32(T), Buf("nrm%d" % i)) for i in range(4)]
        p2_sq = c2.bf16(2, T); p2_sq_ring = Ring([(p2_sq[:, i, :], Buf("p2sq%d" % i)) for i in range(2)])
        p2_uph = c2.f32(2, T + 32); p2_uph_b = Buf("uph"); p2_uph_ch = S.chan("uph")
        p2_pa = [c2.f32(2, T + 32) for _ in range(2)]; p2_pa_b = [Buf("pa0"), Buf("pa1")]
        p2_pp = c2.bf16(2, T); p2_pp_b = Buf("pp")
        p2_chh = c2.f32(2, T + 4); p2_chh_b = Buf("chh"); p2_chh_ch = S.chan("chh")
        p2_bt = c2.f32(2, T); p2_bt_b = Buf("bt"); p2_bt_ch = S.chan("bt")
        p2_q = c2.bf16(4, T); p2_q_b = Buf("q"); p2_q_ch = S.chan("q")
        p2_pt = [c2.bf16(2, T) for _ in range(2)]
        p2_pt_ring = Ring([(p2_pt[i], Buf("pt%d" % i)) for i in range(2)])
        p2_y = c2.bf16(8, T); p2_ypc = p2_y; p2_ypc_b = Buf("y"); p2_yh_b = p2_ypc_b
        p2_tok_ring = Ring([(p2_uph.rearrange("p c w -> p (c w)")[:, 0:D], p2_uph_b, S.chan("otok"))])
        c0 = Carver()
        bdw_f = c0.f32(L, 2, 128)
        ov_all_bufs = []

        ps = es.enter_context(nc.psum_tensor("ps", [P, 8, T], F32))
        ps_b = [Buf("ps%d" % i) for i in range(8)]
        psA = Ring([(ps[:, i, :], ps_b[i]) for i in range(4)])
        psB = Ring([(ps[:, i, :], ps_b[i]) for i in (4, 5)])
        psC = Ring([(ps[:, i, :], ps_b[i]) for i in (6, 7)])
        psBC = Ring([(ps[:, i, :], ps_b[i]) for i in (4, 5, 6, 7)])

        ch_misc = S.chan("misc")
        DBG = {}
        ch_dbg = S.chan("dbg")

        def dbg_dump(name, ap, bufs, dt=F32):
            if not DEBUG or name in DBG:
                return
            shp = list(ap.shape)
            t_ = nc.dram_tensor("dbg_" + name, shp, dt, kind="ExternalOutput").ap()
            DBG[name] = t_
            dma("sp", ch_dbg, t_, ap, bufs, ())

        ch_x = [S.chan("x0"), S.chan("x1")]
        ch_xst = [S.chan("xst0"), S.chan("xst1")]
        ch_cache = S.chan("cache")

        def dma(eng, chan, out, in_, reads=(), writes=(), slow=False):
            if slow:
                S.dma(eng, chan, lambda e: e.dma_start(out=out, in_=in_, allow_slow_non_contiguous=True), reads, writes)
            else:
                S.dma(eng, chan, lambda e: e.dma_start(out=out, in_=in_), reads, writes)

        def mm(out, lhsT, rhs, start, stop, reads, writes, inc, tp=None):
            if tp is None:
                S.op("pe", lambda e: e.matmul(out, lhsT=lhsT, rhs=rhs, start=start, stop=stop), reads, writes, inc)
            else:
                S.op("pe", lambda e: e.matmul(out, lhsT=lhsT, rhs=rhs, start=start, stop=stop, tile_position=tp), reads, writes, inc)

        def tr(out, in_, reads, writes, inc):
            S.op("pe", lambda e: e.transpose(out, in_, ident), list(reads) + [cf_b], writes, inc)

        def act(out, in_, func, reads, writes, **kw):
            S.op("act", lambda e: e.activation(out=out, in_=in_, func=func, **kw), reads, writes)

        def tt(out, in0, in1, op, reads, writes, eng="dve"):
            S.op(eng, lambda e: e.tensor_tensor(out=out, in0=in0, in1=in1, op=op), reads, writes)

        def ts(out, in0, s1, s2, op0, op1, reads, writes, eng="dve"):
            if op1 is None:
                S.op(eng, lambda e: e.tensor_scalar(out=out, in0=in0, scalar1=s1, scalar2=None, op0=op0), reads, writes)
            else:
                S.op(eng, lambda e: e.tensor_scalar(out=out, in0=in0, scalar1=s1, scalar2=s2, op0=op0, op1=op1), reads, writes)

        def stt(out, in0, scalar, in1, op0, op1, reads, writes, eng="dve"):
            S.op(eng, lambda e: e.scalar_tensor_tensor(out=out, in0=in0, scalar=scalar, in1=in1, op0=op0, op1=op1), reads, writes)

        def cp(out, in_, reads, writes, eng="dve"):
            S.op(eng, lambda e: e.tensor_copy(out=out, in_=in_), reads, writes)

        def memset(ap, val, writes, eng="dve"):
            S.op(eng, lambda e: e.memset(ap, val), (), writes)

        def rstd_from(ssum_ap, ssum_b, cadd, out_ap, out_b, npart=P):
            ts(out_ap, ssum_ap, float(cadd), None, ALU.add, None, [ssum_b], [out_b])
            act(out_ap, out_ap, AF.Ln, [out_b], [out_b])
            act(out_ap, out_ap, AF.Exp, [out_b], [out_b], scale=-0.5)

        dma("sp", S.chan("cf"), cf[:], consts_in[:, :], (), [cf_b])
        pl = []

        def pdma(out, in_, slow=True, reads=()):
            b_ = Buf("p%d" % len(pl))
            pl.append(b_)
            dma("sp", ch_misc, out, in_, reads, [b_], slow=slow)
        bdwf_b = Buf("bdwf")
        cp(cb[:, 0:256], cf[:, C_ONES:C_ONES + 256], [cf_b], [cb_b])
        memset(Vg[:], 1.0, KV_b)
        memset(Vd[:], 1.0, KV_b)

        for l in range(L):
            for i, src in enumerate((norm_ffn1, norm_mix, norm_ffn2)):
                pdma(ngain[:, l, i, :], src[l].rearrange("(c p) -> p c", p=P))
            pdma(bada[:, l, :], b_ada[l].rearrange("(m p) -> p m", p=P))
            pdma(pscale[:, l, :], pool_scale[l].rearrange("(c p) -> p c", p=P))
            for k in range(3):
                pdma(convw[:, l, k, :], conv_w[l, k].rearrange("(c p) -> p c", p=P))
        pdma(fgain[:], final_norm.rearrange("(c p) -> p c", p=P))
        for hf in range(2):
            pdma(qkn[hf * 64:(hf + 1) * 64, 0, :], gqa_q_norm.rearrange("l d -> d l"))
            pdma(qkn[hf * 64:(hf + 1) * 64, 1, :], gqa_k_norm.rearrange("l d -> d l"))
        pdma(subln[:], diff_subln.rearrange("l d -> d l"))
        pdma(dl[:], diff_lambda.rearrange("(o l) a d -> o l a d", o=1), slow=False)
        for r_ in range(2):
            pdma(condT[:, :, r_], cond[r_].rearrange("(c p) -> p c", p=P))
        memset(bdw_f, 0.0, [bdwf_b])
        for l in range(L):
            for g in range(4):
                c, hf = g // 2, g % 2
                pdma(bdw_f[hf * 64:(hf + 1) * 64, l, c, hf * 64:(hf + 1) * 64], pool_w[l, g], slow=False, reads=[bdwf_b])
        cp(bdw[:], bdw_f, pl + [bdwf_b], [par_b])
        ts(ngain[:], ngain[:], 32.0, None, ALU.mult, None, pl + [par_b], [par_b])
        ts(fgain[:], fgain[:], 32.0, None, ALU.mult, None, [par_b], [par_b])
        ts(qkn[:], qkn[:], 8.0, None, ALU.mult, None, [par_b], [par_b])
        for l in range(L):
            ts(subln[:, l:l + 1], subln[:, l:l + 1], 8.0 * (1.0 - lam_inits[l]), None, ALU.mult, None, [par_b], [par_b])
        for l in range(L):
            tt(dl2[:, l, 0, :], dl[:, l, 0, :], dl[:, l, 1, :], ALU.mult, [par_b], [par_b])
            tt(dl2[:, l, 1, :], dl[:, l, 2, :], dl[:, l, 3, :], ALU.mult, [par_b], [par_b])
        S.op("dve", lambda e: e.tensor_reduce(out=dl3[:], in_=dl2[:].rearrange("o l a d -> o (l a) d"), axis=AX.X, op=ALU.add), [par_b], [par_b])
        act(dl3[:], dl3[:], AF.Exp, [par_b], [par_b])
        for l in range(L):
            tt(dl3[:, 2 * l:2 * l + 1], dl3[:, 2 * l + 1:2 * l + 2], dl3[:, 2 * l:2 * l + 1], ALU.subtract, [par_b], [par_b])
            ts(dl3[:, 2 * l:2 * l + 1], dl3[:, 2 * l:2 * l + 1], -lam_inits[l], None, ALU.add, None, [par_b], [par_b])
        pst, pst_b = psC.next()
        mm(pst[0:64, 0:2 * L], onesf[0:1, 0:64], dl3[0:1, :], True, True, [cf_b, par_b], [pst_b], True)
        cp(nlam[:], pst[0:64, 0:2 * L].rearrange("p (l a) -> p l a", a=2)[:, :, 0], [pst_b], [par_b])
        act(condS[:], condT[:], AF.Silu, [par_b], [par_b])

        for l in range(L):
            pm, pm_b = psC.next()
            pmv = pm[:, 0:144].rearrange("p (m r) -> p m r", r=2)
            for blk in range(36):
                wbuf, wb_b, wch = wA_ring.next()
                dma("pool", wch, wbuf[:, :, :], w_ada[l, :, blk * 256:(blk + 1) * 256].rearrange("(c p) n -> p c n", p=P), (), [wb_b])
                for j in range(2):
                    m = blk * 2 + j
                    for kc in range(DC):
                        mm(pmv[:, m, :], wbuf[:, kc, j * 128:(j + 1) * 128], condS[:, kc, :], kc == 0, kc == DC - 1,
                           [wb_b, par_b], [pm_b], inc=(kc == DC - 1))
            tt(mods[:, l], pmv, bada[:, l, :].unsqueeze(2).to_broadcast([P, 72, 2]), ALU.add, [pm_b, par_b], [mods_b])
            for i in range(3):
                sh = mods[:, l, (3 * i) * 8:(3 * i + 1) * 8, :]
                sc = mods[:, l, (3 * i + 1) * 8:(3 * i + 2) * 8, :]
                gt = mods[:, l, (3 * i + 2) * 8:(3 * i + 3) * 8, :]
                stt(modA[:, l, i], sc, 1.0, ngain[:, l, i, :].unsqueeze(2).to_broadcast([P, DC, 2]), ALU.add, ALU.mult,
                    [mods_b, par_b], [mods_b])
                ts(modG[:, l, i], gt, 0.5 if i != 1 else 1.0, None, ALU.mult, None, [mods_b], [mods_b])

        def modB(l, i, c, r):
            return mods[:, l, 3 * i * 8 + c, r:r + 1]

        class Tile:
            pass

        tiles_s = []
        for i in range(S_S // T):
            t = Tile(); t.gi = i; t.r = 0; t.kind = "s"; t.loc = i; t.tok0 = i * T
            tiles_s.append(t)
        tiles_p = []
        for i in range(NPS // 2):
            t = Tile(); t.gi = S_S // T + i; t.r = 1; t.kind = "p"; t.loc = i; t.tok0 = S_S + i * T
            tiles_p.append(t)

        def load_x_tile(tile, slot, l):
            if l > 0:
                dma("sp", ch_x[slot], xt[:, slot], XS[:, tile.tok0:tile.tok0 + T].rearrange("(c p) t -> p c t", p=P),
                    [XS_b[tile.gi]], [xt_b[slot]])
                return
            src = xs_in if tile.kind == "s" else xp_in
            base = tile.loc * T
            for blk in range(4):
                xin, xin_b, xin_ch = p1_xin_ring.next()
                dma("sp", xin_ch, xin, src[base + blk * 128: base + (blk + 1) * 128, :], (), [xin_b])
                for half in range(2):
                    pt_, pt_b = psB.next()
                    for c4 in range(4):
                        c = half * 4 + c4
                        tr(pt_[:, c4 * 128:(c4 + 1) * 128], xin[:, c * 128:(c + 1) * 128], [xin_b], [pt_b], inc=(c4 == 3))
                    cp(xt[:, slot, half * 4:half * 4 + 4, blk * 128:(blk + 1) * 128],
                       pt_.rearrange("p (c t) -> p c t", t=128), [pt_b], [xt_b[slot]])

        def store_x_tile(tile, slot):
            dma("sp", ch_xst[slot], XS[:, tile.tok0:tile.tok0 + T].rearrange("(c p) t -> p c t", p=P), xt[:, slot],
                [xt_b[slot]], [XS_b[tile.gi]])

        def norm_mod(tile, slot, l, i, tmp_ring, sq_ring):
            ssp, ss_b = psC.next()
            for c in range(DC):
                sq, sq_b = sq_ring.next()
                act(sq, xt[:, slot, c, :], AF.Square, [xt_b[slot]], [sq_b])
                mm(ssp, ones_b, sq, c == 0, c == DC - 1, [cb_b, sq_b], [ss_b], inc=True)
            rs, rs_b = tmp_ring.next()
            dbg_dump("ss0", ssp, [ss_b]) if False else None
            rstd_from(ssp, ss_b, D * EPS, rs, rs_b)
            dbg_dump("rs0", rs, [rs_b])
            t_alt = [tmp_ring.next(), tmp_ring.next()]
            for c in range(DC):
                t1, t1_b = t_alt[c % 2]
                stt(t1, xt[:, slot, c, :], modA[:, l, i, c, tile.r:tile.r + 1], rs, ALU.mult, ALU.mult,
                    [xt_b[slot], mods_b, rs_b], [t1_b])
                ts(ht[:, slot, c, :], t1, modB(l, i, c, tile.r), None, ALU.add, None, [t1_b, mods_b], [ht_b[slot]])

        def ffn(tiles_slots, l, which):
            i_mod = 0 if which == 0 else 2
            w_in = w_ffn_in[which]; w_out = w_ffn_out[which]
            wts = {}

            def weights(jb):
                if jb not in wts:
                    wg, wg_b, wg_ch = wA_ring.next()
                    wu, wu_b, wu_ch = wA_ring.next()
                    wo, wo_b, wo_ch = wO_ring.next()
                    dma("pool", wg_ch, wg[:, :, :], w_in[l, :, jb * 256:(jb + 1) * 256].rearrange("(c p) n -> p c n", p=P), (), [wg_b])
                    dma("pool", wu_ch, wu[:, :, :], w_in[l, :, FF + jb * 256:FF + (jb + 1) * 256].rearrange("(c p) n -> p c n", p=P), (), [wu_b])
                    dma("pool", wo_ch, wo[:, :, :], w_out[l, jb * 256:(jb + 1) * 256, :].rearrange("(j p) n -> p j n", p=P), (), [wo_b])
                    wts[jb] = (wg, wg_b, wu, wu_b, wo, wo_b)
                return wts[jb]

            units = [(jb, tile, slot) for jb in range(FC // 2) for (tile, slot) in tiles_slots]

            def gu_gen(un):
                jb, tile, slot = un
                wg, wg_b, wu, wu_b, wo, wo_b = weights(jb)
                a, a_b = a_ring.next()
                for j in range(2):
                    pg, pg_b = psA.next()
                    for kc in range(DC):
                        mm(pg, wg[:, kc, j * 128:(j + 1) * 128], ht[:, slot, kc, :], kc == 0, kc == DC - 1,
                           [wg_b, ht_b[slot]], [pg_b], inc=(kc == DC - 1))
                    sg, sg_b = sg_ring.next()
                    act(sg, pg, AF.Silu, [pg_b], [sg_b])
                    yield
                    pu, pu_b = psA.next()
                    for kc in range(DC):
                        mm(pu, wu[:, kc, j * 128:(j + 1) * 128], ht[:, slot, kc, :], kc == 0, kc == DC - 1,
                           [wu_b, ht_b[slot]], [pu_b], inc=(kc == DC - 1))
                    tt(a[:, j, :], sg, pu, ALU.mult, [sg_b, pu_b], [a_b])
                    yield
                return a, a_b

            def down_gen(un, ares):
                jb, tile, slot = un
                wg, wg_b, wu, wu_b, wo, wo_b = weights(jb)
                a, a_b = ares
                for c in range(DC):
                    po, po_b = psBC.next()
                    for j in range(2):
                        mm(po, wo[:, j, c * 128:(c + 1) * 128], a[:, j, :], j == 0, j == 1, [wo_b, a_b], [po_b], inc=(j == 1))
                    stt(xt[:, slot, c, :], po, modG[:, l, i_mod, c, tile.r:tile.r + 1], xt[:, slot, c, :], ALU.mult, ALU.add,
                        [po_b, mods_b, xt_b[slot]], [xt_b[slot]])
                    yield

            def drain(gen):
                while True:
                    try:
                        next(gen)
                    except StopIteration as e_:
                        return e_.value

            a_cur = drain(gu_gen(units[0]))
            for n, un in enumerate(units):
                dg = down_gen(un, a_cur)
                if n + 1 < len(units):
                    gg = gu_gen(units[n + 1])
                    while True:
                        try:
                            next(gg)
                        except StopIteration as e_:
                            a_cur = e_.value
                            break
                        next(dg, None)
                        next(dg, None)
                for _ in dg:
                    pass

        def key_pos(tile):
            return 256 + tile.loc * T if tile.kind == "s" else tile.loc * T

        def head_norm(pin, pin_b, gain_ap, out_ap, out_b):
            sq, sq_b = p1_sq_ring.next()
            act(sq, pin, AF.Square, [pin_b], [sq_b])
            ssp, ss_b = psC.next()
            mm(ssp, bd64_b, sq, True, True, [cb_b, sq_b], [ss_b], inc=True)
            rs, rs_b = p1_tmp_ring.next()
            rstd_from(ssp, ss_b, 64 * EPS, rs, rs_b)
            stt(out_ap, pin, gain_ap, rs, ALU.mult, ALU.mult, [pin_b, par_b, rs_b], [out_b])

        def rope(x_ap, x_b, rot_lhsT, rt, rt_b, out_ap, out_b):
            pr, pr_b = psC.next()
            mm(pr, rot_lhsT, x_ap, True, True, [cf_b, x_b], [pr_b], inc=True)
            t1, t1_b = p1_tmp_ring.next()
            t2, t2_b = p1_tmp_ring.next()
            tt(t1, x_ap, rt[:, 0, :], ALU.mult, [x_b, rt_b], [t1_b])
            tt(t2, pr, rt[:, 1, :], ALU.mult, [pr_b, rt_b], [t2_b])
            tt(out_ap, t1, t2, ALU.add, [t1_b, t2_b], out_b if isinstance(out_b, list) else [out_b])

        def load_rope(tile, which):
            rt, rt_b, rt_ch = p1_rt_ring.next()
            a, b = ("cg", "sg") if which == "g" else ("cd", "sd")
            t0 = tile.loc * T
            dma("sp", rt_ch, rt[:, 0, :], ropes_in[a][:, t0:t0 + T], (), [rt_b])
            dma("sp", rt_ch, rt[:, 1, :], ropes_in[b][:, t0:t0 + T], (), [rt_b])
            return rt, rt_b

        def to_tokmajor(src_ap, src_b, ncols_used=128):
            ptk, ptk_b = psB.next()
            for blk in range(4):
                tr(ptk[:, blk * 128:(blk + 1) * 128], src_ap[:, blk * 128:(blk + 1) * 128], [src_b], [ptk_b], inc=(blk == 3))
            return ptk.rearrange("p (b f) -> p b f", f=128), ptk_b

        def mix_in(tiles_slots, l):
            order = [0, 1, 3, 2, 4, 5, 6, 7, 8]
            for blk in order:
                wbuf, wb_b, wch = wA_ring.next()
                if blk == 4:
                    for j_ in range(2):
                        for i_ in range(2):
                            h_ = 2 * i_ + j_
                            dma("pool", wch, wbuf[:, :, j_ * 128 + i_ * 64:j_ * 128 + (i_ + 1) * 64],
                                w_mix_in[l, :, 1024 + h_ * 64:1024 + (h_ + 1) * 64].rearrange("(c p) d -> p c d", p=P), (), [wb_b])
                else:
                    dma("pool", wch, wbuf[:, :, :], w_mix_in[l, :, blk * 256:(blk + 1) * 256].rearrange("(c p) n -> p c n", p=P), (), [wb_b])
                for ti, (tile, slot) in enumerate(tiles_slots):
                    is_s = tile.kind == "s"
                    kp = key_pos(tile)
                    kvb = [KV_b[kp // 256], KV_b[kp // 256 + 1]]
                    tsl = slice(tile.tok0, tile.tok0 + T)
                    rt = rt_b = None
                    if is_s and blk in (4, 5):
                        rt, rt_b = load_rope(tile, "g")
                    if is_s and blk in (6, 7):
                        rt, rt_b = load_rope(tile, "d")
                    stage = None
                    if blk in (0, 2, 3):
                        stage = p1_stg_ring.next()
                    if blk in (4, 6):
                        stage = p1_stq_ring.next()
                    for j in range(2):
                        pj, pj_b = psA.next()
                        for kc in range(DC):
                            mm(pj, wbuf[:, kc, j * 128:(j + 1) * 128], ht[:, slot, kc, :], kc == 0, kc == DC - 1,
                               [wb_b, ht_b[slot]], [pj_b], inc=(kc == DC - 1))
                        if blk == 0:
                            act(stage[0][:, j, :], pj, AF.Copy, [pj_b], [stage[1]])
                        elif blk == 1:
                            act(p1_hbuf[:, ti, j, :], pj, AF.Copy, [pj_b], [p1_hbuf_b[ti]])
                        elif blk == 3:
                            tt(stage[0][:, j, :], pj, p1_hbuf[:, ti, j, :], ALU.mult, [pj_b, p1_hbuf_b[ti]], [stage[1]])
                        elif blk == 2:
                            act(stage[0][:, j, :], pj, AF.Copy, [pj_b], [stage[1]])
                        elif blk == 4:
                            if is_s:
                                xn, xn_b = p1_tmp_ring.next()
                                head_norm(pj, pj_b, qkn[:, 0, l:l + 1], xn, xn_b)
                                rope(xn, xn_b, rotg, rt, rt_b, stage[0][:, j, :], stage[1])
                            else:
                                head_norm(pj, pj_b, qkn[:, 0, l:l + 1], stage[0][:, j, :], stage[1])
                        elif blk == 5 and j == 0:
                            xn, xn_b = p1_tmp_ring.next()
                            head_norm(pj, pj_b, qkn[:, 1, l:l + 1], xn, xn_b)
                            if is_s:
                                rope(xn, xn_b, rotg, rt, rt_b, KgT[:, kp:kp + T], kvb)
                            else:
                                cp(KgT[:, kp:kp + T], xn, [xn_b], kvb)
                                tk, tk_b = to_tokmajor(xn, xn_b)
                                st, st_b, st_ch = p1_tok_ring.next()
                                cp(st, tk, [tk_b], [st_b])
                                for sq_ in range(2):
                                    seq = tile.loc * 2 + sq_
                                    for hh in range(2):
                                        dma("sp", st_ch, nkg[seq, l, hh].rearrange("(b p) d -> p b d", p=P),
                                            st[:, sq_ * 2:sq_ * 2 + 2, hh * 64:(hh + 1) * 64], [st_b], ())
                        elif (blk == 5 and j == 1) or blk == 8:
                            vf, vf_b = p1_tmp_ring.next()
                            act(vf, pj, AF.Copy, [pj_b], [vf_b])
                            tk, tk_b = to_tokmajor(vf, vf_b)
                            kc0 = kp // 128
                            if blk == 5:
                                cp(Vg[:, kc0:kc0 + 4, :, 0:64], tk.rearrange("p b (h d) -> p b h d", d=64), [tk_b], kvb)
                            else:
                                cp(Vd[:, kc0:kc0 + 4, 2 * j:2 * j + 2, 0:64], tk.rearrange("p b (h d) -> p b h d", d=64), [tk_b], kvb)
                            if not is_s:
                                st, st_b, st_ch = p1_tok_ring.next()
                                cp(st, tk, [tk_b], [st_b])
                                dst = nvg if blk == 5 else nvd
                                for sq_ in range(2):
                                    seq = tile.loc * 2 + sq_
                                    for hh in range(2):
                                        hidx = hh if blk == 5 else 2 * j + hh
                                        dma("sp", st_ch, dst[seq, l, hidx].rearrange("(b p) d -> p b d", p=P),
                                            st[:, sq_ * 2:sq_ * 2 + 2, hh * 64:(hh + 1) * 64], [st_b], ())
                        elif blk == 6:
                            if is_s:
                                xn, xn_b = p1_tmp_ring.next()
                                act(xn, pj, AF.Copy, [pj_b], [xn_b])
                                rope(xn, xn_b, rotd, rt, rt_b, stage[0][:, j, :], stage[1])
                            else:
                                act(stage[0][:, j, :], pj, AF.Copy, [pj_b], [stage[1]])
                        elif blk == 7:
                            xn, xn_b = p1_tmp_ring.next()
                            act(xn, pj, AF.Copy, [pj_b], [xn_b])
                            if is_s:
                                rope(xn, xn_b, rotd, rt, rt_b, KdT[:, j, kp:kp + T], kvb)
                            else:
                                cp(KdT[:, j, kp:kp + T], xn, [xn_b], kvb)
                                tk, tk_b = to_tokmajor(xn, xn_b)
                                st, st_b, st_ch = p1_tok_ring.next()
                                cp(st, tk, [tk_b], [st_b])
                                for sq_ in range(2):
                                    seq = tile.loc * 2 + sq_
                                    for hh in range(2):
                                        dma("sp", st_ch, nkd[seq, l, 2 * j + hh].rearrange("(b p) d -> p b d", p=P),
                                            st[:, sq_ * 2:sq_ * 2 + 2, hh * 64:(hh + 1) * 64], [st_b], ())
                    if blk == 0:
                        dma("sp", stage[2], UPs[:, tsl].rearrange("(c p) t -> p c t", p=P), stage[0], [stage[1]], [SCR["up"][tile.gi]])
                    elif blk == 3:
                        dma("sp", stage[2], CHs[:, tsl].rearrange("(c p) t -> p c t", p=P), stage[0], [stage[1]], [SCR["ch"][tile.gi]])
                    elif blk == 2:
                        dma("sp", stage[2], Bs[:, tsl].rearrange("(c p) t -> p c t", p=P), stage[0], [stage[1]], [SCR["b"][tile.gi]])
                    elif blk == 4:
                        dma("sp", stage[2], QG[:, tsl].rearrange("(c p) t -> p c t", p=P), stage[0], [stage[1]], [SCR["qg"][tile.gi]])
                    elif blk == 6:
                        dma("sp", stage[2], QD[:, tsl].rearrange("(c p) t -> p c t", p=P), stage[0], [stage[1]], [SCR["qd"][tile.gi]])

        def load_cache(l):
            kvb = [KV_b[0]]
            for (src, nh, dstfn) in ((ckg, 2, lambda c: KgT[:, 0:256]), (ckd, 4, lambda c: KdT[:, c, 0:256])):
                for c in range(nh // 2):
                    for blk in range(2):
                        xin, xin_b, xin_ch = p1_xin_ring.next()
                        for hh in range(2):
                            dma("sp", xin_ch, xin[:, hh * 64:(hh + 1) * 64], src[l, 2 * c + hh, blk * 128:(blk + 1) * 128, :], (), [xin_b])
                        ptk, ptk_b = psB.next()
                        tr(ptk[:, 0:128], xin[:, 0:128], [xin_b], [ptk_b], inc=True)
                        cp(dstfn(c)[:, blk * 128:(blk + 1) * 128], ptk[:, 0:128], [ptk_b], kvb)
            for (src, nh, dst) in ((cvg, 2, Vg), (cvd, 4, Vd)):
                for blk in range(2):
                    xin, xin_b, xin_ch = p1_xin_ring.next()
                    dma("sp", xin_ch, xin[:, 0:nh * 64].rearrange("p (h d) -> p h d", d=64),
                        src[l, :, blk * 128:(blk + 1) * 128, :].rearrange("h p d -> p h d"), (), [xin_b])
                    cp(dst[:, blk, :, 0:64], xin[:, 0:nh * 64].rearrange("p (h d) -> p h d", d=64), [xin_b], kvb)

        def phase1(group_tiles, l):
            for i in range(0, len(group_tiles), 2):
                tsl = [(tl, s_) for s_, tl in enumerate(group_tiles[i:i + 2])]
                for (tl, s_) in tsl:
                    load_x_tile(tl, s_, l)
                dbg_dump("x0", xt[:, 0], [xt_b[0]])
                for (tl, s_) in tsl:
                    norm_mod(tl, s_, l, 0, p1_tmp_ring, p1_sq_ring)
                dbg_dump("h0", ht[:, 0], [ht_b[0]], BF16)
                ffn(tsl, l, 0)
                dbg_dump("x1", xt[:, 0], [xt_b[0]])
                for (tl, s_) in tsl:
                    store_x_tile(tl, s_)
                for (tl, s_) in tsl:
                    norm_mod(tl, s_, l, 1, p1_tmp_ring, p1_sq_ring)
                mix_in(tsl, l)

        def pool_conv(tile, l):
            is_s = tile.kind == "s"
            nseg = 1 if is_s else 2
            SL = T // nseg
            HW = SL + 16
            uph = p2_uph[:, :, 0:nseg * HW].rearrange("p c (s w) -> p c s w", w=HW)
            pa = [p2_pa[i][:, :, 0:nseg * HW].rearrange("p c (s w) -> p c s w", w=HW) for i in range(2)]
            left_edge = (not is_s) or tile.loc == 0
            right_edge = (not is_s) or tile.loc == len(tiles_s) - 1
            for c_ in range(2):
                dma("sp", p2_uph_ch, uph[:, c_, :, 8:8 + SL],
                    UPs[c_ * P:(c_ + 1) * P, tile.tok0:tile.tok0 + T].rearrange("p (s t) -> p s t", s=nseg), [SCR["up"][tile.gi]], [p2_uph_b])
            if left_edge:
                memset(uph[:, :, :, 0:8], 0.0, [p2_uph_b])
            else:
                dma("sp", p2_uph_ch, uph[:, :, 0, 0:8], UPs[:, tile.tok0 - 8:tile.tok0].rearrange("(c p) t -> p c t", p=P),
                    [SCR["up"][tile.gi - 1]], [p2_uph_b], slow=True)
            if right_edge:
                memset(uph[:, :, :, 8 + SL:16 + SL], 0.0, [p2_uph_b])
            else:
                dma("sp", p2_uph_ch, uph[:, :, 0, 8 + SL:16 + SL], UPs[:, tile.tok0 + T:tile.tok0 + T + 8].rearrange("(c p) t -> p c t", p=P),
                    [SCR["up"][tile.gi + 1]], [p2_uph_b], slow=True)
            tt(pa[0][:, :, :, 1:HW], uph[:, :, :, 1:HW], uph[:, :, :, 0:HW - 1], ALU.add, [p2_uph_b], [p2_pa_b[0]])
            pp = p2_pp

            def emit_group(g, a_ap, a_b):
                c, hf = g // 2, g % 2
                sl = slice(hf * 64, (hf + 1) * 64)
                out_ = pp[sl, c, :].rearrange("p (s t) -> p s t", s=nseg)
                stt(out_, a_ap[sl, c, :, 8:8 + SL], cf[sl, C_INVW + c:C_INVW + c + 1], uph[sl, c, :, 8:8 + SL], ALU.mult, ALU.subtract,
                    [a_b, cf_b, p2_uph_b], [p2_pp_b])
                if left_edge:
                    for s_ in range(nseg):
                        t1, t1_b = p2_tmp_ring.next()
                        tt(t1[sl, 0:8], a_ap[sl, c, s_, 8:16], cf[sl, C_ICL + c * 8:C_ICL + c * 8 + 8], ALU.mult, [a_b, cf_b], [t1_b])
                        tt(out_[:, s_, 0:8], t1[sl, 0:8], uph[sl, c, s_, 8:16], ALU.subtract, [t1_b, p2_uph_b], [p2_pp_b])
                if right_edge:
                    for s_ in range(nseg):
                        t1, t1_b = p2_tmp_ring.next()
                        tt(t1[sl, 0:8], a_ap[sl, c, s_, SL:SL + 8], cf[sl, C_ICR + c * 8:C_ICR + c * 8 + 8], ALU.mult, [a_b, cf_b], [t1_b])
                        tt(out_[:, s_, SL - 8:SL], t1[sl, 0:8], uph[sl, c, s_, SL:SL + 8], ALU.subtract, [t1_b, p2_uph_b], [p2_pp_b])

            emit_group(0, pa[0], p2_pa_b[0])
            tt(pa[1][:, :, :, 2:HW - 1], pa[0][:, :, :, 3:HW], pa[0][:, :, :, 1:HW - 2], ALU.add, [p2_pa_b[0]], [p2_pa_b[1]])
            emit_group(1, pa[1], p2_pa_b[1])
            tt(pa[0][:, :, :, 4:HW - 3], pa[1][:, :, :, 2:HW - 5], pa[1][:, :, :, 6:HW - 1], ALU.add, [p2_pa_b[1]], [p2_pa_b[0]])
            emit_group(2, pa[0], p2_pa_b[0])
            tt(pa[1][:, :, :, 8:HW - 7], pa[0][:, :, :, 4:HW - 11], pa[0][:, :, :, 12:HW - 3], ALU.add, [p2_pa_b[0]], [p2_pa_b[1]])
            emit_group(3, pa[1], p2_pa_b[1])
            for c in range(2):
                pq, pq_b = psC.next()
                mm(pq, bdw[:, l, c, :], pp[:, c, :], True, True, [par_b, p2_pp_b], [pq_b], inc=True)
                ts(p2_ypc[:, c, :], pq, pscale[:, l, c:c + 1], None, ALU.mult, None, [pq_b, par_b], [p2_ypc_b])
            CW = SL + 2
            chh = p2_chh[:, :, 0:nseg * CW].rearrange("p c (s w) -> p c s w", w=CW)
            for c_ in range(2):
                dma("sp", p2_chh_ch, chh[:, c_, :, 1:1 + SL],
                    CHs[c_ * P:(c_ + 1) * P, tile.tok0:tile.tok0 + T].rearrange("p (s t) -> p s t", s=nseg), [SCR["ch"][tile.gi]], [p2_chh_b])
            if left_edge:
                memset(chh[:, :, :, 0:1], 0.0, [p2_chh_b])
            else:
                dma("sp", p2_chh_ch, chh[:, :, 0, 0:1], CHs[:, tile.tok0 - 1:tile.tok0].rearrange("(c p) t -> p c t", p=P),
                    [SCR["ch"][tile.gi - 1]], [p2_chh_b], slow=True)
            if right_edge:
                memset(chh[:, :, :, SL + 1:SL + 2], 0.0, [p2_chh_b])
            else:
                dma("sp", p2_chh_ch, chh[:, :, 0, SL + 1:SL + 2], CHs[:, tile.tok0 + T:tile.tok0 + T + 1].rearrange("(c p) t -> p c t", p=P),
                    [SCR["ch"][tile.gi + 1]], [p2_chh_b], slow=True)
            dma("sp", p2_bt_ch, p2_bt, Bs[:, tile.tok0:tile.tok0 + T].rearrange("(c p) t -> p c t", p=P), [SCR["b"][tile.gi]], [p2_bt_b])
            for c in range(2):
                acc, acc_b = p2_tmp_ring.next()
                accv = acc.rearrange("p (s t) -> p s t", s=nseg)
                ts(accv, chh[:, c, :, 0:SL], convw[:, l, 0, c:c + 1], None, ALU.mult, None, [p2_chh_b, par_b], [acc_b])
                stt(accv, chh[:, c, :, 1:1 + SL], convw[:, l, 1, c:c + 1], accv, ALU.mult, ALU.add, [p2_chh_b, par_b, acc_b], [acc_b])
                stt(accv, chh[:, c, :, 2:2 + SL], convw[:, l, 2, c:c + 1], accv, ALU.mult, ALU.add, [p2_chh_b, par_b, acc_b], [acc_b])
                tt(p2_ypc[:, 2 + c, :], acc, p2_bt[:, c, :], ALU.mult, [acc_b, p2_bt_b], [p2_ypc_b])

        def attention(tile, l):
            is_s = tile.kind == "s"
            dma("sp", p2_q_ch, p2_q[:, 0:2, :], QG[:, tile.tok0:tile.tok0 + T].rearrange("(c p) t -> p c t", p=P), [SCR["qg"][tile.gi]], [p2_q_b])
            dma("sp", p2_q_ch, p2_q[:, 2:4, :], QD[:, tile.tok0:tile.tok0 + T].rearrange("(c p) t -> p c t", p=P), [SCR["qd"][tile.gi]], [p2_q_b])
            if is_s:
                segs = [(0, T, list(range(NK_S // 128)))]
            else:
                segs = [(0, 256, [tile.loc * 4, tile.loc * 4 + 1]), (256, 256, [tile.loc * 4 + 2, tile.loc * 4 + 3])]
            kv_all = KV_b
            steps = []
            for pi, (kind, idx) in enumerate([("g", 0), ("g", 1), ("d", 0), ("d", 1), ("d", 2), ("d", 3)]):
                units = []
                if kind == "g":
                    for hf in range(2):
                        units.append(dict(qrow=hf * 64, dk=64, qch=idx, kap=lambda k0, hf=hf: KgT[hf * 64:(hf + 1) * 64, k0:k0 + 128],
                                          vap=lambda kc, hf=hf: Vg[:, kc, hf, 0:65], head=idx + 2 * hf))
                    scale = 64 ** -0.5
                else:
                    h = idx
                    for comp in range(2):
                        rb = (h % 2) * 64 + comp * 32
                        units.append(dict(qrow=rb, dk=32, qch=2 + h // 2, kap=lambda k0, rb=rb, h=h: KdT[rb:rb + 32, h // 2, k0:k0 + 128],
                                          vap=lambda kc, h=h: Vd[:, kc, h, 0:65], head=h))
                    scale = 32 ** -0.5
                pc = dict(units=units, scale=scale, kind=kind, idx=idx, o=None, oring=psB,
                          nrm=p2_nrm[(pi % 2) * 2:(pi % 2) * 2 + 2])
                plist = []
                for (q0, qn, kcs) in segs:
                    for ki, kc in enumerate(kcs):
                        plist.append(dict(pc=pc, q0=q0, qn=qn, ki=ki, nk=len(kcs), kc=kc, last=False))
                plist[-1]["last"] = True
                steps += plist

            def qk(st):
                pc = st["pc"]; q0, qn, kc = st["q0"], st["qn"], st["kc"]
                if psA.i % 2:
                    psA.i += 1
                s_aps = [psA.next(), psA.next()]
                bank0 = (psA.i - 2) % 4
                if KEEP_WARM:
                    for _ in range(KEEP_WARM):
                        mm(s_aps[0][0][:, :], ones_b, p2_q[:, 0, :], True, True, [cb_b, p2_q_b], [s_aps[0][1]], inc=False)
                for ui, u in enumerate(pc["units"]):
                    rb, dk = u["qrow"], u["dk"]
                    mm(s_aps[ui][0][:, 0:qn], u["kap"](kc * 128), p2_q[rb:rb + dk, u["qch"], q0:q0 + qn], True, True,
                       kv_all + [p2_q_b], [s_aps[ui][1]], inc=(ui == 1), tp=(rb, 0))
                return s_aps, bank0

            def expv(st, sres):
                pc = st["pc"]; q0, qn, kc, ki, nk = st["q0"], st["qn"], st["kc"], st["ki"], st["nk"]
                s_aps, bank0 = sres
                if pc["o"] is None:
                    pc["o"] = [pc["oring"].next(), pc["oring"].next()]
                pt_, pt_b = p2_pt_ring.next()
                act(pt_[:, :, 0:qn], ps[:, bank0:bank0 + 2, 0:qn], AF.Exp, [s_aps[0][1], s_aps[1][1]], [pt_b], scale=pc["scale"])
                for ui, u in enumerate(pc["units"]):
                    mm(pc["o"][ui][0][0:65, q0:q0 + qn], u["vap"](kc), pt_[:, ui, 0:qn], ki == 0, ki == nk - 1,
                       kv_all + [pt_b], [pc["o"][ui][1]], inc=(ki == nk - 1))

            def norm1(pc):
                for ui in range(2):
                    o_ap, o_b = pc["o"][ui]
                    U, U_b = pc["nrm"][ui]
                    cp(U[64:65, :], o_ap[64:65, :], [o_b], [U_b])

            def norm2(pc):
                for ui, u in enumerate(pc["units"]):
                    o_ap, o_b = pc["o"][ui]
                    U, U_b = pc["nrm"][ui]
                    S.op("dve", lambda e, U=U: e.reciprocal(out=U[64:65, :], in_=U[64:65, :]), [U_b], [U_b])
                    zh, zh_b = p2_sq_ring.next()
                    zl, zl_b = p2_sq_ring.next()
                    cp(zh[64:65, :], U[64:65, :], [U_b], [zh_b])
                    tt(zl[64:65, :], U[64:65, :], zh[64:65, :], ALU.subtract, [U_b, zh_b], [zl_b])
                    pz, pz_b = psC.next()
                    mm(pz[0:64, :], ones_b[64:65, 0:64], zh[64:65, :], True, False, [cb_b, zh_b], [pz_b], inc=False)
                    mm(pz[0:64, :], ones_b[64:65, 0:64], zl[64:65, :], False, True, [cb_b, zl_b], [pz_b], inc=True)
                    cp(U[0:64, :], pz[0:64, :], [pz_b], [U_b])
                    if pc["kind"] == "g":
                        hd = u["head"]
                        tt(p2_y[(hd % 2) * 64:(hd % 2) * 64 + 64, 4 + hd // 2, :], o_ap[0:64, :], U[0:64, :], ALU.mult, [o_b, U_b], [p2_yh_b])
                    else:
                        tt(U[0:64, :], o_ap[0:64, :], U[0:64, :], ALU.mult, [o_b, U_b], [U_b])
                if pc["kind"] == "d":
                    (U1, U1_b), (U2, U2_b) = pc["nrm"]
                    stt(U1[0:64, :], U2[0:64, :], nlam[:, l:l + 1], U1[0:64, :], ALU.mult, ALU.add, [U2_b, par_b, U1_b], [U1_b])
                    sq, sq_b = p2_sq_ring.next()
                    tt(sq[0:64, :], U1[0:64, :], U1[0:64, :], ALU.mult, [U1_b], [sq_b])
                    pc["sq"] = (sq, sq_b)

            def norm3(pc):
                if pc["kind"] != "d":
                    return
                h = pc["idx"]
                (U1, U1_b), (U2, U2_b) = pc["nrm"]
                sq, sq_b = pc["sq"]
                pss, pss_b = psC.next()
                mm(pss[0:64, :], ones_b[0:64, 0:64], sq[0:64, :], True, True, [cb_b, sq_b], [pss_b], inc=True)
                rstd_from(pss[0:64, :], pss_b, 64 * EPS, U2[0:64, :], U2_b)
                stt(p2_y[(h % 2) * 64:(h % 2) * 64 + 64, 6 + h // 2, :], U1[0:64, :], subln[:, l:l + 1], U2[0:64, :], ALU.mult, ALU.mult,
                    [U1_b, par_b, U2_b], [p2_yh_b])

            pending = []
            s_next = qk(steps[0])
            for n, st in enumerate(steps):
                s_cur = s_next
                if n + 1 < len(steps):
                    s_next = qk(steps[n + 1])
                expv(st, s_cur)
                ready = [p_ for p_ in pending if p_[0] <= 0]
                pending = [[p_[0] - 1, p_[1]] for p_ in pending if p_[0] > 0]
                for p_ in ready:
                    p_[1]()
                if st["last"]:
                    pc = st["pc"]
                    norm1(pc)
                    norm2(pc)
                    pending.append([0, lambda pc=pc: norm3(pc)])
            while pending:
                ready = [p_ for p_ in pending if p_[0] <= 0]
                pending = [[p_[0] - 1, p_[1]] for p_ in pending if p_[0] > 0]
                for p_ in ready:
                    p_[1]()

        def mix_out(tiles_done, l):
            raise NotImplementedError

        def mix_out_tile(tile, slot, l):
            for blk in range(4):
                wo, wo_b, wo_ch = wO_ring.next()
                dma("pool", wo_ch, wo[:, :, :], w_mix_out[l, blk * 256:(blk + 1) * 256, :].rearrange("(j p) n -> p j n", p=P), (), [wo_b])
                for c in range(DC):
                    po, po_b = psBC.next()
                    for j in range(2):
                        mm(po, wo[:, j, c * 128:(c + 1) * 128], p2_y[:, blk * 2 + j, :], j == 0, j == 1,
                           [wo_b, p2_ypc_b], [po_b], inc=(j == 1))
                    stt(xt[:, slot, c, :], po, modG[:, l, 1, c, tile.r:tile.r + 1], xt[:, slot, c, :], ALU.mult, ALU.add,
                        [po_b, mods_b, xt_b[slot]], [xt_b[slot]])

        def final_out(tile, slot):
            ssp, ss_b = psC.next()
            for c in range(DC):
                sq, sq_b = p2_sq_ring.next()
                act(sq, xt[:, slot, c, :], AF.Square, [xt_b[slot]], [sq_b])
                mm(ssp, ones_b, sq, c == 0, c == DC - 1, [cb_b, sq_b], [ss_b], inc=True)
            rs, rs_b = p2_tmp_ring.next()
            rstd_from(ssp, ss_b, D * EPS, rs, rs_b)
            for c in range(DC):
                stt(xt[:, slot, c, :], xt[:, slot, c, :], fgain[:, c:c + 1], rs, ALU.mult, ALU.mult,
                    [xt_b[slot], par_b, rs_b], [xt_b[slot]])
            dst = ys_out if tile.kind == "s" else yp_out
            base = tile.loc * T
            for blk in range(4):
                st, st_b, st_ch = p2_tok_ring.next()
                for half in range(2):
                    pt_, pt_b = psA.next()
                    for c4 in range(4):
                        c = half * 4 + c4
                        tr(pt_[:, c4 * 128:(c4 + 1) * 128], xt[:, slot, c, blk * 128:(blk + 1) * 128], [xt_b[slot]], [pt_b], inc=(c4 == 3))
                    cp(st[:, half * 512:(half + 1) * 512], pt_, [pt_b], [st_b])
                dma("sp", st_ch, dst[base + blk * 128:base + (blk + 1) * 128, :], st, [st_b], ())
                out_chans.add(st_ch)

        out_chans = set()

        def phase2(group_tiles, l):
            for i in range(0, len(group_tiles), 2):
                tsl = [(tl, s_) for s_, tl in enumerate(group_tiles[i:i + 2])]
                for (tl, s_) in tsl:
                    load_x_tile(tl, s_, 1)
                    pool_conv(tl, l)
                    attention(tl, l)
                    mix_out_tile(tl, s_, l)
                for (tl, s_) in tsl:
                    norm_mod(tl, s_, l, 2, p2_tmp_ring, p2_sq_ring)
                ffn(tsl, l, 1)
                for (tl, s_) in tsl:
                    if l == L - 1:
                        final_out(tl, s_)
                    else:
                        store_x_tile(tl, s_)

        for (grp, tiles) in (("p", tiles_p), ("s", tiles_s)):
            for l in range(L):
                S.barrier()
                if grp == "s":
                    load_cache(l)
                phase1(tiles, l)
                S.barrier()
                phase2(tiles, l)
        dbg_dump("mods", mods[:], [mods_b])
        dbg_dump("modA", modA[:], [mods_b])
        S.barrier()

        sems = {k: es.enter_context(nc.semaphore(k)) for k in S.sem_keys()}
        with nc.Block() as block:
            S.emit(block, sems)
    return nc, S


_W_NAMES = ["w_ada", "b_ada", "norm_ffn1", "norm_mix", "norm_ffn2", "w_ffn1_in", "w_ffn1_out", "w_ffn2_in",
            "w_ffn2_out", "w_mix_in", "pool_w", "pool_scale", "conv_w", "gqa_q_norm", "gqa_k_norm", "diff_lambda",
            "diff_subln", "w_mix_out", "final_norm"]


def run_cores(inputs, n_cores, trace=False):
    f = lambda a: np.ascontiguousarray(np.asarray(a, dtype=np.float32))
    x_prompt = f(inputs["x_prompt"]); x_sample = f(inputs["x_sample"])
    B, SEQ, _ = x_prompt.shape
    DB, S_S, _ = x_sample.shape
    L = inputs["w_ada"].shape[0]
    assert DB == n_cores and B % n_cores == 0 and SEQ == 256
    NPS = B // n_cores
    nc, S = build_program(L, S_S, NPS)
    cg, sg = _rope_tables(S_S, 64, 2)
    cd, sd = _rope_tables(S_S, 32, 4)
    shared = {k: f(inputs[k]) for k in _W_NAMES}
    shared["consts"] = _const_pack(256 if False else 256)
    shared["rope_cg"] = cg; shared["rope_sg"] = sg; shared["rope_cd"] = cd; shared["rope_sd"] = sd
    ckg = f(inputs["cache_gqa_k"]); cvg = f(inputs["cache_gqa_v"]); ckd = f(inputs["cache_diff_k"]); cvd = f(inputs["cache_diff_v"])
    c = f(inputs["c"]); c_ctx = f(inputs["c_ctx"])
    in_maps = []
    for b in range(n_cores):
        m = dict(shared)
        m["xs"] = x_sample[b]
        m["xp"] = np.ascontiguousarray(x_prompt[b * NPS:(b + 1) * NPS].reshape(NPS * 256, D))
        m["ckg"] = ckg[b]; m["cvg"] = cvg[b]; m["ckd"] = ckd[b]; m["cvd"] = cvd[b]
        m["cond"] = np.ascontiguousarray(np.stack([c[b], c_ctx], axis=0))
        in_maps.append(m)
    res = run_bass_kernel_spmd(nc, in_maps, core_ids=list(range(n_cores)), trace=trace)
    rs = res.results
    global LAST_RESULTS
    LAST_RESULTS = rs
    y_prompt = np.concatenate([r["yp"].reshape(NPS, 256, D) for r in rs], axis=0)
    y_sample = np.stack([r["ys"] for r in rs], axis=0)
    outs = [y_prompt, y_sample]
    for k in ("nkg", "nvg", "nkd", "nvd"):
        outs.append(np.concatenate([r[k] for r in rs], axis=0))
    return tuple(np.ascontiguousarray(o, dtype=np.float32) for o in outs), res


LAST_RESULTS = None


def kernel(**inputs):
    outs, _ = run_cores(inputs, 8)
    return outs
```
